# Optimizing a Trainium2 kernel written in Bass

```python
import math
import jax
import jax.numpy as jnp
from jax import lax
import numpy as np

D_MODEL = 1024
BATCH = 32
SEQ = 2048
DEPTH = 1

N_MEM = 256
D_FF = 2816
NORM_EPS = 1e-6
NEG_INF = -1e30
FORCE_SCORE = 1e4

NSA_HEADS = 8
NSA_GROUPS = 2
NSA_HPG = NSA_HEADS // NSA_GROUPS
NSA_HEAD_DIM = 64
CMP_BLOCK = 32
CMP_STRIDE = 16
CMP_HIDDEN = 256
SEL_BLOCK = 64
SEL_TOPK = 16
WINDOW = 512
WIN_QBLOCK = 128
SEL_QCHUNK = 16

REL_BUCKETS = 32
REL_MAX_DIST = 128

DN_HEADS = 4
DN_HEAD_DIM = 128
DN_CONV = 4
DN_CHUNK = 64

X_HEADS = 4
X_HEAD_DIM = 128

N_BRANCH = 3
NSA_Q = NSA_HEADS * NSA_HEAD_DIM
NSA_KV = NSA_GROUPS * NSA_HEAD_DIM
DN_W = DN_HEADS * DN_HEAD_DIM
X_W = X_HEADS * X_HEAD_DIM
IN_SPLITS = (NSA_Q, 6 * NSA_KV, 3 * NSA_HEADS, 3 * DN_W, DN_HEADS, DN_HEADS, DN_W, X_W)
IN_WIDTH = NSA_Q + 6 * NSA_KV + 3 * NSA_HEADS + 3 * DN_W + 2 * DN_HEADS + DN_W + X_W

kernel_name = "hybrid_nsa_deltanet_memory_macaron_block"


def rms_norm(x, gain):
    xf = x.astype(jnp.float32)
    y = xf * lax.rsqrt(jnp.mean(xf * xf, axis=-1, keepdims=True) + NORM_EPS)
    return (y * gain.astype(jnp.float32)).astype(x.dtype)


def l2_norm(x):
    xf = x.astype(jnp.float32)
    return xf * lax.rsqrt(jnp.sum(xf * xf, axis=-1, keepdims=True) + NORM_EPS)


def swiglu(x, w_gate, w_up, w_down):
    return (jax.nn.silu(x @ w_gate) * (x @ w_up)) @ w_down


def split_cols(a, sizes):
    out, start = [], 0
    for n in sizes:
        out.append(a[..., start:start + n])
        start += n
    return out


def rel_bucket(dist):
    exact = REL_BUCKETS // 2
    d = jnp.maximum(dist, 1).astype(jnp.float32)
    log_bucket = exact + (jnp.log(d / exact) / math.log(REL_MAX_DIST / exact)
                          * (REL_BUCKETS - exact)).astype(jnp.int32)
    return jnp.where(dist < exact, dist, jnp.minimum(log_bucket, REL_BUCKETS - 1))


def masked_softmax(logits, mask):
    return jax.nn.softmax(jnp.where(mask, logits.astype(jnp.float32), NEG_INF), axis=-1)


def compress_blocks(kv, pos_emb, w1, w2):
    b, s, g, dh = kv.shape
    r = CMP_BLOCK // CMP_STRIDE
    n_chunk = s // CMP_STRIDE
    n_cmp = n_chunk - r + 1
    ch = kv.reshape(b, n_chunk, CMP_STRIDE, g, dh)
    blocks = jnp.concatenate([ch[:, j:j + n_cmp] for j in range(r)], axis=2)
    blocks = blocks + pos_emb[:, None, :]
    flat = blocks.transpose(0, 1, 3, 2, 4).reshape(b, n_cmp, g, CMP_BLOCK * dh)
    return jax.nn.silu(flat @ w1) @ w2


def nsa_compressed(q, k_c, v_c, rel_bias):
    seq, n_cmp = q.shape[1], k_c.shape[1]
    t = jnp.arange(seq)
    block_end = jnp.arange(n_cmp) * CMP_STRIDE + CMP_BLOCK - 1
    dist = t[:, None] - block_end[None, :]
    bias = rel_bias[rel_bucket(jnp.maximum(dist, 0))]
    bias = bias.reshape(seq, n_cmp, NSA_GROUPS, NSA_HPG).transpose(2, 3, 0, 1)
    logits = jnp.einsum('bsgjd,bcgd->bgjsc', q, k_c) * NSA_HEAD_DIM ** -0.5 + bias
    has_block = (t >= CMP_BLOCK - 1).astype(jnp.float32)[:, None]
    p = masked_softmax(logits, dist >= 0) * has_block
    out = jnp.einsum('bgjsc,bcgd->bsgjd', p.astype(v_c.dtype), v_c)
    return out, p


def select_blocks(p_cmp, seq):
    n_cmp = p_cmp.shape[-1]
    n_blk = seq // SEL_BLOCK
    n_sel = min(SEL_TOPK, n_blk)
    c0 = jnp.arange(n_cmp) * CMP_STRIDE
    s0 = jnp.arange(n_blk) * SEL_BLOCK
    overlap = jnp.maximum(jnp.minimum(c0[:, None] + CMP_BLOCK, s0[None, :] + SEL_BLOCK)
                          - jnp.maximum(c0[:, None], s0[None, :]), 0)
    to_sel = overlap.astype(jnp.float32) / CMP_BLOCK
    importance = jnp.einsum('bgjsc,cn->bgsn', p_cmp, to_sel)
    cur = jnp.arange(seq) // SEL_BLOCK
    blk = jnp.arange(n_blk)
    forced = (blk[None] == 0) | (blk[None] == cur[:, None]) | (blk[None] == cur[:, None] - 1)
    causal = blk[None] <= cur[:, None]
    score = jnp.where(forced, FORCE_SCORE, jnp.where(causal, importance, -FORCE_SCORE))
    vals, idx = lax.top_k(score, n_sel)
    return idx, vals > -1.0


def nsa_selected(q, k_s, v_s, idx, ok, rel_bias):
    b, s, g, j, dh = q.shape
    n_blk = s // SEL_BLOCK
    n_sel = idx.shape[-1]
    kb = k_s.reshape(b, n_blk, SEL_BLOCK, g, dh).transpose(0, 3, 1, 2, 4)
    vb = v_s.reshape(b, n_blk, SEL_BLOCK, g, dh).transpose(0, 3, 1, 2, 4)
    table_g = rel_bias.reshape(REL_BUCKETS, NSA_GROUPS, NSA_HPG)
    offs = jnp.arange(SEL_BLOCK)

    def one(qg, kg, vg, ix, okg, t, tab):
        ks = kg[ix]
        vs = vg[ix]
        dist = t[:, None, None] - (ix[..., None] * SEL_BLOCK + offs)
        mask = okg[..., None] & (dist >= 0)
        bias = tab[rel_bucket(jnp.maximum(dist, 0))].transpose(0, 3, 1, 2)
        logits = jnp.einsum('qjd,qknd->qjkn', qg, ks) * NSA_HEAD_DIM ** -0.5 + bias
        nq = logits.shape[0]
        p = masked_softmax(logits.reshape(nq, j, n_sel * SEL_BLOCK),
                           mask.reshape(nq, 1, n_sel * SEL_BLOCK)).reshape(nq, j, n_sel, SEL_BLOCK)
        return jnp.einsum('qjkn,qknd->qjd', p.astype(vs.dtype), vs)

    per_g = jax.vmap(one, in_axes=(1, 0, 0, 0, 0, None, 1), out_axes=1)
    per_bg = jax.vmap(per_g, in_axes=(0, 0, 0, 0, 0, None, None), out_axes=0)
    n_chunk = s // SEL_QCHUNK
    qs = q.reshape(b, n_chunk, SEL_QCHUNK, g, j, dh).transpose(1, 0, 2, 3, 4, 5)
    ixs = idx.reshape(b, g, n_chunk, SEL_QCHUNK, n_sel).transpose(2, 0, 1, 3, 4)
    oks = ok.reshape(b, g, n_chunk, SEL_QCHUNK, n_sel).transpose(2, 0, 1, 3, 4)
    ts = jnp.arange(s).reshape(n_chunk, SEL_QCHUNK)

    def chunk(args):
        qc, ixc, okc, tc = args
        return per_bg(qc, kb, vb, ixc, okc, tc, table_g)

    out = lax.map(chunk, (qs, ixs, oks, ts))
    return out.transpose(1, 0, 2, 3, 4, 5).reshape(b, s, g, j, dh)


def nsa_window(q, k_w, v_w, rel_bias):
    b, s, g, j, dh = q.shape
    n_qb = s // WIN_QBLOCK
    span = WINDOW + WIN_QBLOCK
    pad = ((0, 0), (WINDOW, 0), (0, 0), (0, 0))
    kp = jnp.pad(k_w, pad)
    vp = jnp.pad(v_w, pad)
    qi = jnp.arange(WIN_QBLOCK)
    kj = jnp.arange(span)
    rel = qi[:, None] + WINDOW - kj[None, :]
    bias = rel_bias[rel_bucket(jnp.maximum(rel, 0))]
    bias = bias.reshape(WIN_QBLOCK, span, g, j).transpose(2, 3, 0, 1)
    band = (rel >= 0) & (rel < WINDOW)

    def block(i):
        start = i * WIN_QBLOCK
        qb = lax.dynamic_slice_in_dim(q, start, WIN_QBLOCK, axis=1)
        kb = lax.dynamic_slice_in_dim(kp, start, span, axis=1)
        vb = lax.dynamic_slice_in_dim(vp, start, span, axis=1)
        mask = band & ((start - WINDOW + kj) >= 0)[None, :]
        logits = jnp.einsum('bqgjd,bkgd->bgjqk', qb, kb) * NSA_HEAD_DIM ** -0.5 + bias
        p = masked_softmax(logits, mask)
        return jnp.einsum('bgjqk,bkgd->bqgjd', p.astype(vb.dtype), vb)

    out = lax.map(block, jnp.arange(n_qb))
    return out.transpose(1, 0, 2, 3, 4, 5).reshape(b, s, g, j, dh)


def causal_depthwise_conv(x, w):
    return lax.conv_general_dilated(
        x, w[:, None, :].astype(x.dtype), window_strides=(1,), padding=[(w.shape[0] - 1, 0)],
        dimension_numbers=('NWC', 'WIO', 'NWC'), feature_group_count=x.shape[-1])


def gated_delta_rule(q, k, v, log_decay, beta):
    f32 = jnp.float32
    b, s, h, dk = q.shape
    dv = v.shape[-1]
    c = DN_CHUNK
    n = s // c

    def chunks(a):
        a = a.astype(f32).reshape((b, n, c, h) + a.shape[3:])
        return jnp.moveaxis(a, 3, 1)

    qc = chunks(q) * dk ** -0.5
    kc, vc = chunks(k), chunks(v)
    gc = jnp.cumsum(chunks(log_decay), axis=-1)
    bc = chunks(beta)
    causal = jnp.tril(jnp.ones((c, c), bool))
    strict = jnp.tril(jnp.ones((c, c), bool), -1)
    decay = jnp.exp(jnp.where(causal, gc[..., :, None] - gc[..., None, :], -jnp.inf))
    kk = jnp.einsum('bhnid,bhnjd->bhnij', kc, kc)
    a_mat = jnp.where(strict, bc[..., :, None] * kk * decay, 0.0) + jnp.eye(c, dtype=f32)
    rhs = jnp.concatenate([vc * bc[..., None], kc * (bc * jnp.exp(gc))[..., None]], axis=-1)
    sol = lax.linalg.triangular_solve(a_mat, rhs, left_side=True, lower=True, unit_diagonal=True)
    u, w = sol[..., :dv], sol[..., dv:]
    qk = jnp.einsum('bhnid,bhnjd->bhnij', qc, kc) * decay
    q_dec = qc * jnp.exp(gc)[..., None]
    k_dec = kc * jnp.exp(gc[..., -1:] - gc)[..., None]
    g_last = jnp.exp(gc[..., -1])

    def step(state, inp):
        qk_n, qd_n, kd_n, u_n, w_n, gl_n = inp
        v_new = u_n - jnp.einsum('bhck,bhkv->bhcv', w_n, state)
        o = jnp.einsum('bhck,bhkv->bhcv', qd_n, state) + jnp.einsum('bhcm,bhmv->bhcv', qk_n, v_new)
        state = state * gl_n[..., None, None] + jnp.einsum('bhck,bhcv->bhkv', kd_n, v_new)
        return state, o

    xs = tuple(jnp.moveaxis(a, 2, 0) for a in (qk, q_dec, k_dec, u, w, g_last))
    _, o = lax.scan(step, jnp.zeros((b, h, dk, dv), f32), xs)
    return o.transpose(1, 0, 3, 2, 4).reshape(b, s, h, dv)


def memory_cross_attention(q, mem_n, w_mem_kv):
    b, m, _ = mem_n.shape
    kv = (mem_n @ w_mem_kv).reshape(b, m, 2, X_HEADS, X_HEAD_DIM)
    k, v = kv[:, :, 0], kv[:, :, 1]
    logits = jnp.einsum('bshd,bmhd->bhsm', q, k) * X_HEAD_DIM ** -0.5
    p = jax.nn.softmax(logits.astype(jnp.float32), axis=-1).astype(v.dtype)
    return jnp.einsum('bhsm,bmhd->bshd', p, v)


def setup_inputs(seed: int = 0) -> dict:
    key = jax.random.key(seed)
    keys = iter(jax.random.split(key, 48))
    f32 = jnp.float32
    L, D = DEPTH, D_MODEL

    def normal(shape, scale):
        return jax.random.normal(next(keys), shape, f32) * scale

    def gain(width):
        return 1.0 + 0.02 * jax.random.normal(next(keys), (L, width), f32)

    dt = jnp.exp(jax.random.uniform(next(keys), (L, DN_HEADS), f32, math.log(1e-3), math.log(1e-1)))
    a_log = jnp.log(jax.random.uniform(next(keys), (L, DN_HEADS), f32, 1.0, 16.0))
    return {
        "x": normal((BATCH, SEQ, D), 1.0),
        "mem": normal((BATCH, N_MEM, D), 1.0),
        "ffn1_pre_norm": gain(D),
        "ffn1_w_gate": normal((L, D, D_FF), D ** -0.5),
        "ffn1_w_up": normal((L, D, D_FF), D ** -0.5),
        "ffn1_w_down": normal((L, D_FF, D), D_FF ** -0.5),
        "ffn1_post_norm": gain(D),
        "mix_pre_norm": gain(D),
        "w_in": normal((L, D, IN_WIDTH), D ** -0.5),
        "cmp_pos_k": normal((L, CMP_BLOCK, NSA_HEAD_DIM), 0.1),
        "cmp_pos_v": normal((L, CMP_BLOCK, NSA_HEAD_DIM), 0.1),
        "cmp_k_w1": normal((L, CMP_BLOCK * NSA_HEAD_DIM, CMP_HIDDEN), (CMP_BLOCK * NSA_HEAD_DIM) ** -0.5),
        "cmp_k_w2": normal((L, CMP_HIDDEN, NSA_HEAD_DIM), CMP_HIDDEN ** -0.5),
        "cmp_v_w1": normal((L, CMP_BLOCK * NSA_HEAD_DIM, CMP_HIDDEN), (CMP_BLOCK * NSA_HEAD_DIM) ** -0.5),
        "cmp_v_w2": normal((L, CMP_HIDDEN, NSA_HEAD_DIM), CMP_HIDDEN ** -0.5),
        "rel_bias": normal((REL_BUCKETS, NSA_HEADS), 0.5),
        "dn_conv_w": normal((L, DN_CONV, 3 * DN_W), DN_CONV ** -0.5),
        "dn_a_log": a_log,
        "dn_dt_bias": dt + jnp.log(-jnp.expm1(-dt)),
        "dn_out_norm": gain(DN_HEAD_DIM),
        "mem_norm": gain(D),
        "w_mem_kv": normal((L, D, 2 * X_W), D ** -0.5),
        "w_branch_nsa": normal((L, NSA_Q, D), NSA_Q ** -0.5),
        "w_branch_dn": normal((L, DN_W, D), DN_W ** -0.5),
        "w_branch_mem": normal((L, X_W, D), X_W ** -0.5),
        "w_branch_gate": normal((L, D, N_BRANCH * D), D ** -0.5),
        "w_out": normal((L, D, D), D ** -0.5),
        "mix_post_norm": gain(D),
        "ffn2_pre_norm": gain(D),
        "ffn2_w_gate": normal((L, D, D_FF), D ** -0.5),
        "ffn2_w_up": normal((L, D, D_FF), D ** -0.5),
        "ffn2_w_down": normal((L, D_FF, D), D_FF ** -0.5),
        "ffn2_post_norm": gain(D),
    }


def reference(x, mem, ffn1_pre_norm, ffn1_w_gate, ffn1_w_up, ffn1_w_down, ffn1_post_norm,
              mix_pre_norm, w_in, cmp_pos_k, cmp_pos_v, cmp_k_w1, cmp_k_w2, cmp_v_w1, cmp_v_w2,
              rel_bias, dn_conv_w, dn_a_log, dn_dt_bias, dn_out_norm, mem_norm, w_mem_kv,
              w_branch_nsa, w_branch_dn, w_branch_mem, w_branch_gate, w_out, mix_post_norm,
              ffn2_pre_norm, ffn2_w_gate, ffn2_w_up, ffn2_w_down, ffn2_post_norm):
    b, s, d = x.shape
    f32 = jnp.float32
    for l in range(DEPTH):
        h = swiglu(rms_norm(x, ffn1_pre_norm[l]), ffn1_w_gate[l], ffn1_w_up[l], ffn1_w_down[l])
        x = x + 0.5 * rms_norm(h, ffn1_post_norm[l])

        u = rms_norm(x, mix_pre_norm[l])
        nsa_q, nsa_kv, nsa_g, dn_qkv, dn_a, dn_b, dn_z, mem_q = split_cols(u @ w_in[l], IN_SPLITS)

        q = nsa_q.reshape(b, s, NSA_GROUPS, NSA_HPG, NSA_HEAD_DIM)
        kv = nsa_kv.reshape(b, s, 6, NSA_GROUPS, NSA_HEAD_DIM)
        k_cmp = compress_blocks(kv[:, :, 0], cmp_pos_k[l], cmp_k_w1[l], cmp_k_w2[l])
        v_cmp = compress_blocks(kv[:, :, 1], cmp_pos_v[l], cmp_v_w1[l], cmp_v_w2[l])
        o_cmp, p_cmp = nsa_compressed(q, k_cmp, v_cmp, rel_bias)
        sel_idx, sel_ok = select_blocks(p_cmp, s)
        o_sel = nsa_selected(q, kv[:, :, 2], kv[:, :, 3], sel_idx, sel_ok, rel_bias)
        o_win = nsa_window(q, kv[:, :, 4], kv[:, :, 5], rel_bias)
        g = jax.nn.sigmoid(nsa_g).reshape(b, s, 3, NSA_GROUPS, NSA_HPG, 1)
        o_nsa = (g[:, :, 0] * o_cmp + g[:, :, 1] * o_sel + g[:, :, 2] * o_win).reshape(b, s, NSA_Q)

        qkv = jax.nn.silu(causal_depthwise_conv(dn_qkv, dn_conv_w[l]))
        dq, dk, dv = split_cols(qkv, (DN_W, DN_W, DN_W))
        dq = l2_norm(dq.reshape(b, s, DN_HEADS, DN_HEAD_DIM))
        dk = l2_norm(dk.reshape(b, s, DN_HEADS, DN_HEAD_DIM))
        dv = dv.reshape(b, s, DN_HEADS, DN_HEAD_DIM)
        log_decay = -jnp.exp(dn_a_log[l].astype(f32)) * jax.nn.softplus((dn_a + dn_dt_bias[l]).astype(f32))
        beta = jax.nn.sigmoid(dn_b.astype(f32))
        o_dn = gated_delta_rule(dq, dk, dv, log_decay, beta).astype(x.dtype)
        o_dn = rms_norm(o_dn, dn_out_norm[l]) * jax.nn.silu(dn_z.reshape(b, s, DN_HEADS, DN_HEAD_DIM))
        o_dn = o_dn.reshape(b, s, DN_W)

        o_mem = memory_cross_attention(mem_q.reshape(b, s, X_HEADS, X_HEAD_DIM),
                                       rms_norm(mem, mem_norm[l]), w_mem_kv[l]).reshape(b, s, X_W)

        gates = jax.nn.sigmoid(u @ w_branch_gate[l]).reshape(b, s, N_BRANCH, d)
        merged = (gates[:, :, 0] * (o_nsa @ w_branch_nsa[l])
                  + gates[:, :, 1] * (o_dn @ w_branch_dn[l])
                  + gates[:, :, 2] * (o_mem @ w_branch_mem[l]))
        x = x + rms_norm(merged @ w_out[l], mix_post_norm[l])

        h = swiglu(rms_norm(x, ffn2_pre_norm[l]), ffn2_w_gate[l], ffn2_w_up[l], ffn2_w_down[l])
        x = x + 0.5 * rms_norm(h, ffn2_post_norm[l])
    return x
```

```python
import contextlib
import numpy as np
import ml_dtypes
import concourse.bass as bass
import concourse.mybir as mybir
from concourse.bass_utils import run_bass_kernel_spmd

F32 = mybir.dt.float32
BF16 = mybir.dt.bfloat16
AF = mybir.ActivationFunctionType
ALU = mybir.AluOpType
AX = mybir.AxisListType

D = 1024
SEQ = 2048
DFF = 2816
NFF = DFF // 128
EPS = 1e-6
NDS = 40


class Dep:
    __slots__ = ("w", "r", "excl")

    def __init__(self, excl=False):
        self.w = None
        self.r = {}
        self.excl = excl


class Prog:
    def __init__(self, nc, es):
        self.nc = nc
        self.names = ["pe", "act", "dve", "pool", "sp"]
        self.sem = {n: es.enter_context(nc.semaphore("s_" + n)) for n in self.names}
        self.cnt = {n: 0 for n in self.names}
        self.q = {n: [] for n in self.names}
        self.seen = {n: {} for n in self.names}
        self.dsems = [es.enter_context(nc.semaphore("d%d" % i)) for i in range(NDS)]
        self.dval = [0] * NDS
        self.dnext = 0
        self.out_tokens = []

    def _waits(self, eng, reads, writes, is_dma):
        need = {}

        def add(t, same_ok):
            if t is None:
                return
            sem, val, src = t
            if same_ok and src == eng:
                return
            k = id(sem)
            if k not in need or need[k][1] < val:
                need[k] = (sem, val)

        for d in reads:
            add(d.w, False)
        for d in writes:
            add(d.w, not is_dma)
            for t in d.r.values():
                add(t, not is_dma)
        out = []
        for k, (sem, val) in need.items():
            if self.seen[eng].get(k, 0) >= val:
                continue
            self.seen[eng][k] = val
            out.append((sem, val))
        return out

    def op(self, eng, fn, reads=(), writes=()):
        rx = [d for d in reads if d.excl and d not in writes]
        if rx:
            writes = list(writes) + rx
        waits = self._waits(eng, reads, writes, False)
        self.cnt[eng] += 1
        tok = (self.sem[eng], self.cnt[eng], eng)
        self.q[eng].append((waits, fn, self.sem[eng], 1))
        for d in reads:
            d.r[id(tok[0])] = tok
        for d in writes:
            d.w = tok
            d.r = {}
        return tok

    def dma(self, eng, fn, reads=(), writes=(), is_output=False):
        i = self.dnext
        self.dnext = (i + 1) % NDS
        waits = self._waits(eng, reads, writes, True)
        ds = self.dsems[i]
        if self.dval[i] > 0 and self.seen[eng].get(id(ds), 0) < self.dval[i]:
            waits.append((ds, self.dval[i]))
            self.seen[eng][id(ds)] = self.dval[i]
        self.dval[i] += 16
        tok = (ds, self.dval[i], "dma")
        self.q[eng].append((waits, fn, ds, 16))
        for d in reads:
            d.r[id(ds)] = tok
        for d in writes:
            d.w = tok
            d.r = {}
        if is_output:
            self.out_tokens.append(tok)
        return tok

    def barrier(self):
        for e in self.names:
            waits = []
            for f in self.names:
                if f == e or self.cnt[f] == 0:
                    continue
                k = id(self.sem[f])
                if self.seen[e].get(k, 0) < self.cnt[f]:
                    self.seen[e][k] = self.cnt[f]
                    waits.append((self.sem[f], self.cnt[f]))
            for i in range(NDS):
                k = id(self.dsems[i])
                if self.dval[i] > 0 and self.seen[e].get(k, 0) < self.dval[i]:
                    self.seen[e][k] = self.dval[i]
                    waits.append((self.dsems[i], self.dval[i]))
            if waits:
                self.q[e].append((waits, None, None, 0))

    def finish(self):
        waits = {}
        for sem, val, _ in self.out_tokens:
            k = id(sem)
            if k not in waits or waits[k][1] < val:
                waits[k] = (sem, val)
        self.q["sp"].append((list(waits.values()), None, None, 0))

    def emit(self):
        nc = self.nc
        with nc.Block() as block:

            def replay(e, name):
                for waits, fn, sem, inc in self.q[name]:
                    if fn is None:
                        for s, v in waits:
                            e.wait_ge(s, v)
                        continue
                    for s, v in waits:
                        e.wait_ge(s, v)
                    fn(e).then_inc(sem, inc)

            @block.sync
            def _(e):
                replay(e, "sp")

            @block.tensor
            def _(e):
                replay(e, "pe")

            @block.scalar
            def _(e):
                replay(e, "act")

            @block.vector
            def _(e):
                replay(e, "dve")

            @block.gpsimd
            def _(e):
                replay(e, "pool")


class Buf:
    def __init__(self, t, excl=False):
        self.t = t
        self.deps = {}
        self.excl = excl

    def d(self, key=0):
        if key not in self.deps:
            self.deps[key] = Dep(self.excl)
        return self.deps[key]

    def all(self):
        return list(self.deps.values())

    def __getitem__(self, k):
        return self.t[k]


def dram_rows_bcast(ap2d_row, nparts):
    return ap2d_row.partition_broadcast(nparts)


def ffn_phase(P, nc, es, x_src, x_dst, w_gate, w_up, w_down, g_pre, g_post, ident, ps, ntok, tagp):
    sb = lambda name, shape, dt: Buf(es.enter_context(nc.sbuf_tensor(tagp + name, shape, dt)))
    wg = sb("wg", [128, 8, DFF], BF16)
    wu = sb("wu", [128, 8, DFF], BF16)
    wd = sb("wd", [128, NFF, D], BF16)
    gpre = sb("gpre", [128, D], F32)
    gpost = sb("gpost", [128, D], F32)
    xnT = [sb("xnT%d" % i, [128, 8, 512], BF16) for i in range(2)]
    actT = sb("actT", [128, NFF, 512], BF16)
    xt = [sb("xt%d" % i, [128, D], F32) for i in range(2)]
    xs = [sb("xs%d" % i, [128, D], BF16) for i in range(2)]
    sg = [sb("sg%d" % i, [128, 512], BF16) for i in range(2)]
    ob = [sb("ob%d" % i, [128, D], F32) for i in range(2)]
    xr = [sb("xr%d" % i, [128, D], F32) for i in range(1)]
    junk = sb("junk", [128, D], BF16)
    st = [sb("st%d" % i, [128, 8], F32) for i in range(4)]

    wgv = w_gate.rearrange("(c p) f -> p c f", p=128)
    wuv = w_up.rearrange("(c p) f -> p c f", p=128)
    wdv = w_down.rearrange("(c p) f -> p c f", p=128)
    P.dma("sp", lambda e: e.dma_start(out=gpre[:, :], in_=g_pre.partition_broadcast(128)), writes=[gpre.d()])
    P.dma("sp", lambda e: e.dma_start(out=gpost[:, :], in_=g_post.partition_broadcast(128)), writes=[gpost.d()])
    for c in range(8):
        P.dma("pool", lambda e, c=c: e.dma_start(out=wg[:, c, :], in_=wgv[:, c, :]), writes=[wg.d(c)])
        P.dma("pool", lambda e, c=c: e.dma_start(out=wu[:, c, :], in_=wuv[:, c, :]), writes=[wu.d(c)])
    for f in range(NFF):
        P.dma("pool", lambda e, f=f: e.dma_start(out=wd[:, f, :], in_=wdv[:, f, :]), writes=[wd.d(f)])

    ngrp = ntok // 512
    nst = [0]

    def prep_group(g):
        xb = xnT[g % 2]
        for tt in range(4):
            r0 = g * 512 + tt * 128
            i = (g * 4 + tt) % 2
            s = st[(g * 4 + tt) % 4]
            P.dma("sp", lambda e, i=i, r0=r0: e.dma_start(out=xt[i][:, :], in_=x_src[r0:r0 + 128, :]), writes=[xt[i].d()])
            P.op("act", lambda e, i=i, s=s: e.activation(out=junk[:, :], in_=xt[i][:, :], func=AF.Square, accum_out=s[:, 0:1]),
                 reads=[xt[i].d()], writes=[junk.d(), s.d()])
            P.op("dve", lambda e, s=s: e.tensor_scalar(out=s[:, 1:2], in0=s[:, 0:1], scalar1=1.0 / D, scalar2=EPS, op0=ALU.mult, op1=ALU.add),
                 reads=[s.d()], writes=[s.d()])
            P.op("pool", lambda e, s=s: e.tensor_tensor(out=s[:, 3:4], in0=s[:, 1:2], in1=G["nhalf"][:, 0:1], op=ALU.pow),
                 reads=[s.d(), G["nhalf"].d()], writes=[s.d()])
            P.op("dve", lambda e, i=i, s=s: e.scalar_tensor_tensor(out=xs[i][:, :], in0=xt[i][:, :], scalar=s[:, 3:4], in1=gpre[:, :], op0=ALU.mult, op1=ALU.mult),
                 reads=[xt[i].d(), s.d(), gpre.d()], writes=[xs[i].d()])
            pb = ps[3]
            pv = pb[:, 0:512].bitcast(BF16)
            for c in range(8):
                P.op("pe", lambda e, i=i, c=c, pv=pv: e.transpose(out=pv[:, c * 128:(c + 1) * 128], in_=xs[i][:, c * 128:(c + 1) * 128], identity=ident[:, :]),
                     reads=[xs[i].d(), ident.d()], writes=[pb.d(0)])
            P.op("dve", lambda e, xb=xb, tt=tt, pv=pv: e.tensor_copy(out=xb[:, :, tt * 128:(tt + 1) * 128], in_=pv.rearrange("p (c t) -> p c t", c=8)),
                 reads=[pb.d(0)], writes=[xb.d()])

    def gate_up(g):
        xb = xnT[g % 2]
        for f in range(NFF):
            pg = ps[0] if f % 2 == 0 else ps[1]
            for c in range(8):
                P.op("pe", lambda e, pg=pg, c=c, f=f, xb=xb: e.matmul(pg[:, 0:512], lhsT=wg[:, c, f * 128:(f + 1) * 128], rhs=xb[:, c, :], start=(c == 0), stop=(c == 7)),
                     reads=[wg.d(c), xb.d()], writes=[pg.d(0)])
            for c in range(8):
                P.op("pe", lambda e, pg=pg, c=c, f=f, xb=xb: e.matmul(pg[:, 512:1024], lhsT=wu[:, c, f * 128:(f + 1) * 128], rhs=xb[:, c, :], start=(c == 0), stop=(c == 7)),
                     reads=[wu.d(c), xb.d()], writes=[pg.d(1)])
            sgb = sg[f % 2]
            P.op("act", lambda e, pg=pg, sgb=sgb: e.activation(out=sgb[:, :], in_=pg[:, 0:512], func=AF.Silu), reads=[pg.d(0)], writes=[sgb.d()])
            P.op("dve", lambda e, pg=pg, sgb=sgb, f=f: e.tensor_tensor(out=actT[:, f, :], in0=pg[:, 512:1024], in1=sgb[:, :], op=ALU.mult),
                 reads=[pg.d(1), sgb.d()], writes=[actT.d(f)])

    def down(g):
        for tt in range(4):
            r0 = g * 512 + tt * 128
            pd = ps[2] if tt % 2 == 0 else ps[0]
            for h in range(2):
                for f in range(NFF):
                    P.op("pe", lambda e, pd=pd, h=h, f=f, tt=tt: e.matmul(pd[:, h * 512:(h + 1) * 512], lhsT=actT[:, f, tt * 128:(tt + 1) * 128], rhs=wd[:, f, h * 512:(h + 1) * 512], start=(f == 0), stop=(f == NFF - 1)),
                         reads=[actT.d(f), wd.d(f)], writes=[pd.d(h)])
            s = st[(g * 4 + tt) % 4]
            o = ob[(g * 4 + tt) % 2]
            P.dma("pool", lambda e, r0=r0: e.dma_start(out=xr[0][:, :], in_=x_src[r0:r0 + 128, :]), writes=[xr[0].d()])
            P.op("act", lambda e, pd=pd, o=o: e.copy(out=o[:, :], in_=pd[:, :]), reads=[pd.d(0), pd.d(1)], writes=[o.d()])
            P.op("act", lambda e, o=o, s=s: e.activation(out=junk[:, :], in_=o[:, :], func=AF.Square, accum_out=s[:, 4:5]),
                 reads=[o.d()], writes=[junk.d(), s.d()])
            P.op("dve", lambda e, s=s: e.tensor_scalar(out=s[:, 5:6], in0=s[:, 4:5], scalar1=1.0 / D, scalar2=EPS, op0=ALU.mult, op1=ALU.add),
                 reads=[s.d()], writes=[s.d()])
            P.op("pool", lambda e, s=s: e.tensor_tensor(out=s[:, 7:8], in0=s[:, 5:6], in1=G["nhalf"][:, 0:1], op=ALU.pow),
                 reads=[s.d(), G["nhalf"].d()], writes=[s.d()])
            P.op("dve", lambda e, s=s, o=o: e.scalar_tensor_tensor(out=o[:, :], in0=o[:, :], scalar=s[:, 7:8], in1=gpost[:, :], op0=ALU.mult, op1=ALU.mult),
                 reads=[o.d(), s.d(), gpost.d()], writes=[o.d()])
            P.op("dve", lambda e, o=o: e.scalar_tensor_tensor(out=o[:, :], in0=o[:, :], scalar=0.5, in1=xr[0][:, :], op0=ALU.mult, op1=ALU.add),
                 reads=[o.d(), xr[0].d()], writes=[o.d()])
            P.dma("pool", lambda e, o=o, r0=r0: e.dma_start(out=x_dst[r0:r0 + 128, :], in_=o[:, :]), reads=[o.d()], writes=[], is_output=True)

    prep_group(0)
    for g in range(ngrp):
        gate_up(g)
        if g + 1 < ngrp:
            prep_group(g + 1)
        down(g)


UID = [0]


class Ops:
    def __init__(self, P):
        self.P = P

    def MM(self, out, lhsT, rhs, st, sp, r, w):
        self.P.op("pe", lambda e: e.matmul(out, lhsT=lhsT, rhs=rhs, start=st, stop=sp), r, w)

    def TR(self, out, in_, ident, r, w):
        self.P.op("pe", lambda e: e.transpose(out=out, in_=in_, identity=ident), r, w)

    def ACT(self, out, in_, func, r, w, bias=None, scale=None, accum=None):
        kw = {}
        if bias is not None:
            kw["bias"] = bias
        if scale is not None:
            kw["scale"] = scale
        if accum is not None:
            kw["accum_out"] = accum
        self.P.op("act", lambda e: e.activation(out=out, in_=in_, func=func, **kw), r, w)

    def TT(self, eng, out, in0, in1, op, r, w):
        self.P.op(eng, lambda e: e.tensor_tensor(out=out, in0=in0, in1=in1, op=op), r, w)

    def TS(self, eng, out, in0, s1, s2, op0, op1, r, w):
        if s2 is None:
            self.P.op(eng, lambda e: e.tensor_scalar(out=out, in0=in0, scalar1=s1, scalar2=None, op0=op0), r, w)
        else:
            self.P.op(eng, lambda e: e.tensor_scalar(out=out, in0=in0, scalar1=s1, scalar2=s2, op0=op0, op1=op1), r, w)

    def STT(self, eng, out, in0, scalar, in1, op0, op1, r, w):
        self.P.op(eng, lambda e: e.scalar_tensor_tensor(out=out, in0=in0, scalar=scalar, in1=in1, op0=op0, op1=op1), r, w)

    def CP(self, eng, out, in_, r, w):
        if eng == "act":
            self.P.op(eng, lambda e: e.copy(out=out, in_=in_), r, w)
        else:
            self.P.op(eng, lambda e: e.tensor_copy(out=out, in_=in_), r, w)

    def MS(self, eng, ap, val, w):
        self.P.op(eng, lambda e: e.memset(ap, val), [], w)

    def RCP(self, out, in_, r, w):
        self.P.op("dve", lambda e: e.reciprocal(out=out, in_=in_), r, w)

    def DMA(self, q, out, in_, r, w, is_output=False):
        self.P.dma(q, lambda e: e.dma_start(out=out, in_=in_), r, w, is_output=is_output)


G = {}


def rstd_ops(O, s, i_ss, i_tmp, i_sq, i_out, n, r_extra=()):
    nh = G["nhalf"]
    O.TS("dve", s[:, i_tmp], s[:, i_ss], 1.0 / n, EPS, ALU.mult, ALU.add, [s.d()], [s.d()])
    O.TT("pool", s[:, i_out], s[:, i_tmp], nh[:, 0:1], ALU.pow, [s.d(), nh.d()], [s.d()])


def setup_consts(P, O, nc, es, CD, ident):
    sbp = lambda name, shape, dt: Buf(es.enter_context(nc.sbuf_tensor(name, shape, dt)))
    C = {"ident": ident}
    C["cf"] = cf = sbp("cf", [128, 5, 128], F32)
    C["T1d"] = T1d = sbp("T1d", [128, 8, 256], BF16)
    C["Gc"] = Gc = sbp("Gc", [128, 8, 247], BF16)
    C["Eb"] = Eb = sbp("Eb", [32, 2048], BF16)
    C["band"] = band = sbp("band", [128, 128], BF16)
    C["ub"] = ub = sbp("ub", [128, 8, 32], F32)
    C["lb"] = lb = sbp("lb", [128, 8, 32], F32)
    C["tab"] = tab = sbp("tab", [128, 256], F32)
    C["tosel"] = tosel = sbp("tosel", [128, 32], BF16)
    O.DMA("sp", cf[:, :, :], CD["c_f32"], [], [cf.d()])
    O.DMA("sp", Eb[:, :], CD["c_E"], [], [Eb.d()])
    O.DMA("sp", band[:, :], CD["c_band"], [], [band.d()])
    O.DMA("sp", ub[:, :, :], CD["c_ub"], [], [ub.d()])
    O.DMA("sp", lb[:, :, :], CD["c_lb"], [], [lb.d()])
    O.DMA("sp", tosel[:, :], CD["c_tosel"], [], [tosel.d()])
    O.DMA("sp", tab[:, :], CD["rel_bias"].partition_broadcast(128), [], [tab.d()])
    with contextlib.ExitStack() as et:
        sbt = lambda name, shape, dt: Buf(et.enter_context(nc.sbuf_tensor(name, shape, dt)))
        idx1 = sbt("idx1", [128, 256], F32)
        mask1 = sbt("mask1", [128, 256], F32)
        idxc = sbt("idxc", [128, 247], F32)
        maskc = sbt("maskc", [128, 247], F32)
        tabd = sbt("tabd", [128, 256], F32)
        oh1 = sbt("oh1", [128, 256], F32)
        ohc = sbt("ohc", [128, 247], F32)
        acc1 = sbt("acc1", [128, 8, 256], F32)
        accc = sbt("accc", [128, 8, 247], F32)
        O.DMA("sp", idx1[:, :], CD["c_idx1"], [], [idx1.d()])
        O.DMA("sp", mask1[:, :], CD["c_mask1"], [], [mask1.d()])
        O.DMA("sp", idxc[:, :], CD["c_idxc"], [], [idxc.d()])
        O.DMA("sp", maskc[:, :], CD["c_maskc"], [], [maskc.d()])
        t3 = tab[:, :].rearrange("p (b h) -> p b h", h=8)
        O.TT("dve", tabd[:, :].rearrange("p (b h) -> p b h", h=8), t3, tab[:, 248:256].unsqueeze(1).to_broadcast([128, 32, 8]), ALU.subtract,
             [tab.d()], [tabd.d()])
        for h in range(8):
            O.CP("dve", acc1[:, h, :], mask1[:, :], [mask1.d()], [acc1.d()])
            O.CP("dve", accc[:, h, :], maskc[:, :], [maskc.d()], [accc.d()])
        for b in range(31):
            O.TS("dve", oh1[:, :], idx1[:, :], float(b), None, ALU.is_equal, None, [idx1.d()], [oh1.d()])
            O.TS("dve", ohc[:, :], idxc[:, :], float(b), None, ALU.is_equal, None, [idxc.d()], [ohc.d()])
            for h in range(8):
                O.STT("dve", acc1[:, h, :], oh1[:, :], tabd[:, b * 8 + h:b * 8 + h + 1], acc1[:, h, :], ALU.mult, ALU.add,
                      [oh1.d(), tabd.d(), acc1.d()], [acc1.d()])
                O.STT("dve", accc[:, h, :], ohc[:, :], tabd[:, b * 8 + h:b * 8 + h + 1], accc[:, h, :], ALU.mult, ALU.add,
                      [ohc.d(), tabd.d(), accc.d()], [accc.d()])
        O.CP("dve", T1d[:, :, :], acc1[:, :, :], [acc1.d()], [T1d.d()])
        O.CP("dve", Gc[:, :, :], accc[:, :, :], [accc.d()], [Gc.d()])
        P.barrier()
    return C


class PsumAlloc:
    def __init__(self, ps):
        self.ps = ps
        self.ctr = {}
        self.mode = "att"
        self.maps = {"att": {"S": [(0, 0), (0, 1), (2, 0)], "O": [(1, 0), (1, 1)], "T": [(2, 1)], "M": [(3, 0), (3, 1)]},
                     "att2": {"S": [(0, 0), (0, 1), (2, 0), (3, 0)], "O": [(1, 0), (1, 1)], "T": [(2, 1)], "M": [(3, 1)]},
                     "dn": {"S": [(0, 0), (0, 1)], "O": [(1, 0), (1, 1)], "T": [(2, 0), (2, 1)], "M": [(3, 0), (3, 1)]}}

    def get(self, pool):
        banks = self.maps[self.mode][pool]
        k = self.ctr.get(pool, 0)
        self.ctr[pool] = k + 1
        b, h = banks[k % len(banks)]
        buf = self.ps[b]
        return buf[:, h * 512:(h + 1) * 512], buf.d(h)


def norm_rows_T(P, O, nc, PA, C, tmp, src_rows_ap, gain, dstT, col0, k):
    xt = tmp["xt"][k % len(tmp["xt"])]
    xs = tmp["xs"][k % len(tmp["xs"])]
    s = tmp["st"][k % 4]
    junk = tmp["junk"]
    ident = C["ident"]
    O.DMA("sp", xt[:, :], src_rows_ap, [], [xt.d()])
    O.ACT(junk[:, :], xt[:, :], AF.Square, [xt.d()], [junk.d(), s.d()], accum=s[:, 0:1])
    rstd_ops(O, s, slice(0, 1), slice(1, 2), slice(2, 3), slice(3, 4), D)
    O.STT("dve", xs[:, :], xt[:, :], s[:, 3:4], gain[:, :], ALU.mult, ALU.mult, [xt.d(), s.d(), gain.d()], [xs.d()])
    pb, pd = PA.get("T")
    pv = pb.bitcast(BF16)
    for c in range(8):
        O.TR(pv[:, c * 128:(c + 1) * 128], xs[:, c * 128:(c + 1) * 128], ident[:, :], [xs.d(), ident.d()], [pd])
    O.CP("dve", dstT[:, :, col0:col0 + 128], pv.rearrange("p (c t) -> p c t", c=8), [pd], [dstT.d(col0 // 128)])


def nsa_phase(P, O, nc, PA, C, W, uT, onsaT, s, dbg=None):
    ident, T1d, Gc, Eb, band, ub, lb, tab, tosel = (C[k] for k in ("ident", "T1d", "Gc", "Eb", "band", "ub", "lb", "tab", "tosel"))
    w_inv = W["w_in"].rearrange("(c p) f -> p c f", p=128)
    uall = [uT.d(k) for k in range(16)]
    ug = lambda tg: [uT.d(4 * tg + k) for k in range(4)]
    UID[0] += 1
    with contextlib.ExitStack() as esA:
        sbA = lambda name, shape, dt: Buf(esA.enter_context(nc.sbuf_tensor("A_%d_" % UID[0] + name, shape, dt)))
        winA = sbA("winA", [128, 8, 1304], BF16)
        for c in range(8):
            O.DMA("pool", winA[:, c, :], w_inv[:, c, 0:1304], [], [winA.d(c)])
        wA = [winA.d(c) for c in range(8)]
        kcmpT = sbA("kcmpT", [65, 2, 128], BF16)
        VcX = sbA("VcX", [128, 2, 97], BF16)
        with contextlib.ExitStack() as esA1:
            sb1 = lambda name, shape, dt: Buf(esA1.enter_context(nc.sbuf_tensor("A1_%d_" % UID[0] + name, shape, dt)))
            kvT = [sb1("kcT", [64, 2, 2048], BF16), sb1("vcT", [64, 2, 2048], BF16)]
            w1 = [sb1("w1k", [64, 32, 256], BF16), sb1("w1v", [64, 32, 256], BF16)]
            w2 = [sb1("w2k", [128, 2, 64], BF16), sb1("w2v", [128, 2, 64], BF16)]
            pos = [sb1("posk", [64, 32], BF16), sb1("posv", [64, 32], BF16)]
            hb = sb1("hb", [128, 4], F32)
            hT = sb1("hT", [128, 8, 127], BF16)
            for kind, nm in enumerate(("k", "v")):
                O.DMA("pool", w1[kind][:, :, :], W["cmp_%s_w1" % nm].rearrange("(j d) h -> d j h", d=64), [], [w1[kind].d()])
                O.DMA("pool", w2[kind][:, :, :], W["cmp_%s_w2" % nm].rearrange("(c p) f -> p c f", p=128), [], [w2[kind].d()])
                O.DMA("pool", pos[kind][:, :], W["cmp_pos_%sT" % nm], [], [pos[kind].d()])
            for kind in range(2):
                col0 = 512 + kind * 128
                for g in range(2):
                    for tg in range(4):
                        pb, pd = PA.get("M")
                        for c in range(8):
                            O.MM(pb[0:64, :], winA[:, c, col0 + g * 64:col0 + g * 64 + 64], uT[:, c, tg * 512:(tg + 1) * 512], c == 0, c == 7,
                                 [winA.d(c)] + ug(tg), [pd])
                        O.CP("act", kvT[kind][0:64, g, tg * 512:(tg + 1) * 512], pb[0:64, :], [pd], [kvT[kind].d()])
            for kind in range(2):
                for ch in range(2):
                    pb, pd = PA.get("M")
                    for j in range(32):
                        O.MM(pb[:, 0:1], w1[kind][0:64, j, ch * 128:(ch + 1) * 128], pos[kind][0:64, j:j + 1], j == 0, j == 31,
                             [w1[kind].d(), pos[kind].d()], [pd])
                    O.CP("dve", hb[:, kind * 2 + ch:kind * 2 + ch + 1], pb[:, 0:1], [pd], [hb.d()])
            for kind in range(2):
                for g in range(2):
                    for ch in range(2):
                        pb, pd = PA.get("M")
                        for j in range(32):
                            O.MM(pb[:, 0:127], w1[kind][0:64, j, ch * 128:(ch + 1) * 128], kvT[kind][0:64, g, j:j + 2017:16], j == 0, j == 31,
                                 [w1[kind].d(), kvT[kind].d()], [pd])
                        O.ACT(hT[:, kind * 4 + g * 2 + ch, :], pb[:, 0:127], AF.Silu, [pd, hb.d()], [hT.d()],
                              bias=hb[:, kind * 2 + ch:kind * 2 + ch + 1])
            for g in range(2):
                pb, pd = PA.get("M")
                for ch in range(2):
                    O.MM(pb[0:64, 0:127], w2[0][:, ch, :], hT[:, g * 2 + ch, :], ch == 0, ch == 1, [w2[0].d(), hT.d()], [pd])
                O.CP("dve", kcmpT[0:64, g, 0:127], pb[0:64, 0:127], [pd], [kcmpT.d()])
                pb, pd = PA.get("M")
                for ch in range(2):
                    O.MM(pb[0:127, 0:64], hT[:, 4 + g * 2 + ch, :], w2[1][:, ch, :], ch == 0, ch == 1, [w2[1].d(), hT.d()], [pd])
                O.CP("dve", VcX[0:127, g, 0:64], pb[0:127, 0:64], [pd], [VcX.d()])
                O.CP("dve", VcX[0:127, g, 65:97], tosel[0:127, :], [tosel.d()], [VcX.d()])
            O.MS("dve", kcmpT[64:65, :, :], 1.0, [kcmpT.d()])
            O.MS("dve", VcX[:, :, 64:65], 1.0, [VcX.d()])
            P.barrier()
        qT = sbA("qT", [65, 8, 2048], BF16)
        ksT = sbA("ksT", [65, 2, 2048], BF16)
        kwT = sbA("kwT", [65, 2, 2048], BF16)
        Vs1 = sbA("Vs1", [128, 16, 2, 65], BF16)
        Vw1 = sbA("Vw1", [128, 16, 2, 65], BF16)
        gs = sbA("gs", [128, 16, 24], F32)
        for h in range(8):
            for tg in range(4):
                pb, pd = PA.get("M")
                for c in range(8):
                    O.MM(pb[0:64, :], winA[:, c, h * 64:(h + 1) * 64], uT[:, c, tg * 512:(tg + 1) * 512], c == 0, c == 7, [winA.d(c)] + ug(tg), [pd])
                P.op("act", lambda e, h=h, tg=tg, pb=pb: e.mul(out=qT[0:64, h, tg * 512:(tg + 1) * 512], in_=pb[0:64, :], mul=0.125), [pd], [qT.d(tg)])
            O.TS("dve", qT[64:65, h, :], uT[64:65, 0, :], 0.0, tab[64:65, 248 + h:249 + h], ALU.mult, ALU.add, uall + [tab.d()],
                 [qT.d(k) for k in range(4)])
        for dst, col0 in ((ksT, 768), (kwT, 1024)):
            for g in range(2):
                for tg in range(4):
                    pb, pd = PA.get("M")
                    for c in range(8):
                        O.MM(pb[0:64, :], winA[:, c, col0 + g * 64:col0 + g * 64 + 64], uT[:, c, tg * 512:(tg + 1) * 512], c == 0, c == 7,
                             [winA.d(c)] + ug(tg), [pd])
                    O.CP("act", dst[0:64, g, tg * 512:(tg + 1) * 512], pb[0:64, :], [pd], [dst.d()])
            O.MS("dve", dst[64:65, :, :], 1.0, [dst.d()])
        for t in range(16):
            pb, pd = PA.get("M")
            for c in range(8):
                O.MM(pb[:, 0:408], uT[:, c, t * 128:(t + 1) * 128], winA[:, c, 896:1304], c == 0, c == 7, [winA.d(c), uT.d(t)], [pd])
            O.CP("dve", Vs1[:, t, :, 0:64], pb[:, 0:128].rearrange("p (g d) -> p g d", g=2), [pd], [Vs1.d()])
            O.CP("dve", Vw1[:, t, :, 0:64], pb[:, 256:384].rearrange("p (g d) -> p g d", g=2), [pd], [Vw1.d()])
            O.ACT(gs[:, t, :], pb[:, 384:408], AF.Sigmoid, [pd], [gs.d()])
        O.MS("dve", Vs1[:, :, :, 64:65], 1.0, [Vs1.d()])
        O.MS("dve", Vw1[:, :, :, 64:65], 1.0, [Vw1.d()])
        PT = [sbA("PT%d" % k, [128, 512], BF16) for k in range(5)]
        onsa = [sbA("onsa%d" % k, [128, 8, 64], F32) for k in range(2)]
        onsab = [sbA("onsab%d" % k, [128, 512], BF16) for k in range(2)]
        sm = [sbA("sm%d" % k, [128, 16], F32) for k in range(4)]
        tmpo = [sbA("tmpo%d" % k, [128, 4, 64], F32) for k in range(2)]
        timp = sbA("timp", [128, 4, 32], F32)
        sc = [sbA("sc%d" % k, [128, 32], F32) for k in range(3)]
        m8 = sbA("m8", [128, 16], F32)
        negsel = [sbA("negsel%d" % k, [128, 32], BF16) for k in range(2)]
        nsT = [sbA("nsT%d" % k, [32, 128], BF16) for k in range(2)]
        ptc = [0]
        smc = [0]

        def post(po, pod, width, i, br, g, o_acc, first):
            pov = po[:, 0:4 * width].rearrange("p (h w) -> p h w", h=4)
            s_ = sm[smc[0] % 4]
            smc[0] += 1
            O.TS("dve", s_[:, 0:4].unsqueeze(2), pov[:, :, 64:65], 1e-30, None, ALU.max, None, [pod], [s_.d()])
            O.RCP(s_[:, 4:8], s_[:, 0:4], [s_.d()], [s_.d()])
            O.TT("dve", s_[:, 8:12], s_[:, 4:8], gs[:, i, br * 8 + 4 * g:br * 8 + 4 * g + 4], ALU.mult, [s_.d(), gs.d()], [s_.d()])
            fb = s_[:, 8:12].unsqueeze(2).to_broadcast([128, 4, 64])
            if first:
                O.TT("dve", o_acc[:, 4 * g:4 * g + 4, :], pov[:, :, 0:64], fb, ALU.mult, [pod, s_.d()], [o_acc.d(g)])
            else:
                t_ = tmpo[smc[0] % 2]
                O.TT("dve", t_[:, :, :], pov[:, :, 0:64], fb, ALU.mult, [pod, s_.d()], [t_.d()])
                O.TT("dve", o_acc[:, 4 * g:4 * g + 4, :], o_acc[:, 4 * g:4 * g + 4, :], t_[:, :, :], ALU.add, [t_.d(), o_acc.d(g)], [o_acc.d(g)])
            return s_, pov

        steps = []
        defer = {}

        def add_defer(idx, fn):
            defer.setdefault(idx, []).append(fn)

        def mk_cmp(i, g, oa, qd):
            st = {}

            def qk():
                pS, pSd = PA.get("S")
                st["pS"] = (pS, pSd)
                O.MM(pS[0:127, :].rearrange("p (h q) -> p h q", h=4), kcmpT[0:65, g, 0:127], qT[0:65, 4 * g:4 * g + 4, i * 128:(i + 1) * 128], True, False,
                     [kcmpT.d()] + qd, [pSd])
                for hh in range(4):
                    O.MM(pS[0:127, hh * 128:(hh + 1) * 128], Gc[:, 4 * g + hh, 120 - 8 * i:247 - 8 * i], ident[:, :], False, hh == 3, [Gc.d(), ident.d()], [pSd])

            def act():
                pS, pSd = st["pS"]
                pt = PT[ptc[0] % len(PT)]
                ptc[0] += 1
                st["pt"] = pt
                O.ACT(pt[0:127, :], pS[0:127, :], AF.Exp, [pSd], [pt.d()])

            def pv(k):
                pt = st["pt"]
                po, pod = PA.get("O")
                for hh in range(4):
                    O.MM(po[:, hh * 97:(hh + 1) * 97], pt[0:127, hh * 128:(hh + 1) * 128], VcX[0:127, g, :], True, True, [pt.d(), VcX.d()], [pod])
                s_, pov = post(po, pod, 97, i, 0, g, oa, True)
                if i >= 8:
                    O.TT("dve", timp[:, :, :], pov[:, :, 65:97], s_[:, 4:8].unsqueeze(2).to_broadcast([128, 4, 32]), ALU.mult, [pod, s_.d()], [timp.d()])
                    P.op("dve", lambda e: e.tensor_reduce(out=sc[0][:, :], in_=timp[:, :, :].rearrange("p h n -> p n h"), axis=AX.X, op=ALU.add),
                         [timp.d()], [sc[0].d()])
                    O.TT("dve", sc[1][:, :], sc[0][:, :], ub[:, i - 8, :], ALU.min, [sc[0].d(), ub.d()], [sc[1].d()])
                    O.TT("dve", sc[1][:, :], sc[1][:, :], lb[:, i - 8, :], ALU.max, [sc[1].d(), lb.d()], [sc[1].d()])
                    P.op("dve", lambda e: e.max(out=m8[:, 0:8], in_=sc[1][:, :]), [sc[1].d()], [m8.d()])
                    P.op("dve", lambda e: e.match_replace(out=sc[2][:, :], in_to_replace=m8[:, 0:8], in_values=sc[1][:, :], imm_value=-1e9),
                         [sc[1].d(), m8.d()], [sc[2].d()])
                    P.op("dve", lambda e: e.max(out=m8[:, 8:16], in_=sc[2][:, :]), [sc[2].d()], [m8.d()])
                    ng = negsel[g]
                    O.TS("dve", ng[:, :], sc[1][:, :], m8[:, 15:16], -30000.0, ALU.is_lt, ALU.mult, [sc[1].d(), m8.d()], [ng.d()])

                    def tr():
                        pb, pd = PA.get("T")
                        pvw = pb.bitcast(BF16)
                        O.TR(pvw[0:32, 0:128], ng[:, :], ident[:, :], [ng.d(), ident.d()], [pd])
                        O.CP("dve", nsT[g][:, :], pvw[0:32, 0:128], [pd], [nsT[g].d()])
                    add_defer(k + 4, tr)

            return (qk, act, pv)

        def mk_att(i, g, br, kT_, V1_, j, idx, njs, grp, oa, qd):
            st = {}

            def qk():
                pS, pSd = PA.get("S")
                st["pS"] = (pS, pSd)
                pS3 = pS.rearrange("p (h q) -> p h q", h=4)
                extra = []
                if j == i:
                    extra.append((ident[:, :], T1d[:, 4 * g:4 * g + 4, 0:128], [ident.d(), T1d.d()]))
                if j == i - 1:
                    extra.append((ident[:, :], T1d[:, 4 * g:4 * g + 4, 128:256], [ident.d(), T1d.d()]))
                if br == 1 and i >= 8:
                    extra.append((Eb[0:32, j * 128:(j + 1) * 128], nsT[g][0:32, :].unsqueeze(1).to_broadcast([32, 4, 128]), [Eb.d(), nsT[g].d()]))
                if br == 2 and j == i - 4:
                    extra.append((ident[:, :], band[:, :].unsqueeze(1).to_broadcast([128, 4, 128]), [ident.d(), band.d()]))
                O.MM(pS3, kT_[0:65, g, j * 128:(j + 1) * 128], qT[0:65, 4 * g:4 * g + 4, i * 128:(i + 1) * 128], True, len(extra) == 0,
                     [kT_.d()] + qd, [pSd])
                for k2, (l_, r_, dd) in enumerate(extra):
                    O.MM(pS3, l_, r_, False, k2 == len(extra) - 1, dd, [pSd])

            def act():
                pS, pSd = st["pS"]
                pt = PT[ptc[0] % len(PT)]
                ptc[0] += 1
                st["pt"] = pt
                O.ACT(pt[:, :], pS, AF.Exp, [pSd], [pt.d()])

            def pv(k):
                pt = st["pt"]
                if idx == 0:
                    grp["po"] = PA.get("O")
                po, pod = grp["po"]
                for hh in range(4):
                    O.MM(po[:, hh * 65:(hh + 1) * 65], pt[:, hh * 128:(hh + 1) * 128], V1_[:, j, g, :], idx == 0 and hh == 0, idx == njs - 1,
                         [pt.d(), V1_.d()], [pod])
                if idx == njs - 1:
                    post(po, pod, 65, i, br, g, oa, False)

            return (qk, act, pv)

        def mk_fin(i, oa):
            def fin():
                ob_ = onsab[i % 2]
                O.CP("act", ob_[:, :], oa[:, :, :].rearrange("p h d -> p (h d)"), [oa.d(0), oa.d(1)], [ob_.d()])
                pb, pd = PA.get("T")
                pvw = pb.bitcast(BF16)
                for c in range(4):
                    O.TR(pvw[:, c * 128:(c + 1) * 128], ob_[:, c * 128:(c + 1) * 128], ident[:, :], [ob_.d(), ident.d()], [pd])
                O.CP("dve", onsaT[:, :, i * 128:(i + 1) * 128], pvw[:, 0:512].rearrange("p (c t) -> p c t", c=4), [pd], [onsaT.d(i)])
            return fin

        for i in range(16):
            oa = onsa[i % 2]
            qd = [qT.d(i // 4)]
            for g in range(2):
                steps.append(mk_cmp(i, g, oa, qd))
            for br, kT_, V1_ in ((2, kwT, Vw1), (1, ksT, Vs1)):
                for g in range(2):
                    js = list(range(max(0, i - 4), i + 1)) if br == 2 else list(range(0, i + 1))
                    grp = {}
                    for idx, j in enumerate(js):
                        steps.append(mk_att(i, g, br, kT_, V1_, j, idx, len(js), grp, oa, qd))
            add_defer(len(steps) - 1 + 3, mk_fin(i, oa))
        LA = 3
        PA.mode = "att2"
        nst = len(steps)
        for k in range(min(LA, nst)):
            steps[k][0]()
        for k in range(nst):
            if k + LA < nst:
                steps[k + LA][0]()
            steps[k][1]()
            steps[k][2](k)
            for fn in defer.pop(k, []):
                fn()
        for k in sorted(defer):
            for fn in defer[k]:
                fn()
        PA.mode = "att"
        P.barrier()


def mem_phase(P, O, nc, PA, C, W, uT, omemT, mem_rows, after_weights=None):
    ident = C["ident"]
    w_inv = W["w_in"].rearrange("(c p) f -> p c f", p=128)
    wkv_v = W["w_mem_kv"].rearrange("(c p) f -> p c f", p=128)
    ug = lambda tg: [uT.d(4 * tg + k) for k in range(4)]
    UID[0] += 1
    with contextlib.ExitStack() as esB:
        sbB = lambda name, shape, dt: Buf(esB.enter_context(nc.sbuf_tensor("B_%d_" % UID[0] + name, shape, dt)))
        wkv = sbB("wkv", [128, 8, 1024], BF16)
        winM = sbB("winM", [128, 8, 512], BF16)
        for c in range(8):
            O.DMA("pool", wkv[:, c, :], wkv_v[:, c, :], [], [wkv.d(c)])
            O.DMA("pool", winM[:, c, :], w_inv[:, c, 3360:3872], [], [winM.d(c)])
        if after_weights is not None:
            after_weights()
        gmem = sbB("gmem", [128, D], F32)
        O.DMA("sp", gmem[:, :], W["mem_norm"].partition_broadcast(128), [], [gmem.d()])
        tmp = {"xt": [sbB("xt%d" % i, [128, D], F32) for i in range(2)], "xs": [sbB("xs%d" % i, [128, D], BF16) for i in range(1)],
               "st": [sbB("st%d" % i, [128, 8], F32) for i in range(4)], "junk": sbB("junk", [128, D], BF16)}
        memnT = sbB("memnT", [128, 8, 256], BF16)
        for t in range(2):
            norm_rows_T(P, O, nc, PA, C, tmp, mem_rows[t * 128:(t + 1) * 128, :], gmem, memnT, t * 128, t)
        md = [memnT.d(0), memnT.d(1)]
        kmT = sbB("kmT", [128, 4, 256], BF16)
        Vm1 = sbB("Vm1", [128, 2, 4, 129], BF16)
        mqT = sbB("mqT", [128, 4, 2048], BF16)
        for h in range(4):
            pb, pd = PA.get("M")
            for c in range(8):
                O.MM(pb[:, 0:256], wkv[:, c, h * 128:(h + 1) * 128], memnT[:, c, :], c == 0, c == 7, [wkv.d(c)] + md, [pd])
            O.CP("dve", kmT[:, h, :], pb[:, 0:256], [pd], [kmT.d()])
        for mt in range(2):
            pb, pd = PA.get("M")
            for c in range(8):
                O.MM(pb[:, :], memnT[:, c, mt * 128:(mt + 1) * 128], wkv[:, c, 512:1024], c == 0, c == 7, [wkv.d(c)] + md, [pd])
            O.CP("dve", Vm1[:, mt, :, 0:128], pb.rearrange("p (h d) -> p h d", h=4), [pd], [Vm1.d()])
        O.MS("dve", Vm1[:, :, :, 128:129], 1.0, [Vm1.d()])
        for h in range(4):
            for tg in range(4):
                pb, pd = PA.get("M")
                for c in range(8):
                    O.MM(pb[:, :], winM[:, c, h * 128:(h + 1) * 128], uT[:, c, tg * 512:(tg + 1) * 512], c == 0, c == 7, [winM.d(c)] + ug(tg), [pd])
                P.op("act", lambda e, h=h, tg=tg, pb=pb: e.mul(out=mqT[:, h, tg * 512:(tg + 1) * 512], in_=pb[:, :], mul=128 ** -0.5), [pd], [mqT.d(tg)])
        omem = [sbB("omem%d" % k, [128, 4, 128], BF16) for k in range(4)]
        PTm = [sbB("PTm%d" % k, [128, 512], BF16) for k in range(4)]
        sm = [sbB("sm%d" % k, [128, 4], F32) for k in range(4)]
        k_ = 0
        for tg in range(4):
            for h in range(4):
                pts = []
                for mt in range(2):
                    pS, pSd = PA.get("S")
                    O.MM(pS, kmT[:, h, mt * 128:(mt + 1) * 128], mqT[:, h, tg * 512:(tg + 1) * 512], True, True, [kmT.d(), mqT.d(tg)], [pSd])
                    pt = PTm[(h * 2 + mt) % 4]
                    O.ACT(pt[:, :], pS, AF.Exp, [pSd], [pt.d()])
                    pts.append(pt)
                for qt in range(4):
                    po, pod = PA.get("O")
                    for mt in range(2):
                        O.MM(po[:, 0:129], pts[mt][:, qt * 128:(qt + 1) * 128], Vm1[:, mt, h, :], mt == 0, mt == 1, [pts[mt].d(), Vm1.d()], [pod])
                    s_ = sm[k_ % 4]
                    k_ += 1
                    O.RCP(s_[:, 0:1], po[:, 128:129], [pod], [s_.d()])
                    O.TS("dve", omem[qt][:, h, :], po[:, 0:128], s_[:, 0:1], None, ALU.mult, None, [pod, s_.d()], [omem[qt].d()])
            for qt in range(4):
                t = tg * 4 + qt
                pb, pd = PA.get("T")
                pv = pb.bitcast(BF16)
                om = omem[qt][:, :, :].rearrange("p h d -> p (h d)")
                for c in range(4):
                    O.TR(pv[:, c * 128:(c + 1) * 128], om[:, c * 128:(c + 1) * 128], ident[:, :], [omem[qt].d(), ident.d()], [pd])
                O.CP("dve", omemT[:, :, t * 128:(t + 1) * 128], pv[:, 0:512].rearrange("p (c t) -> p c t", c=4), [pd], [omemT.d(t)])
        P.barrier()


def merge_phase(P, O, nc, PA, C, W, uT, obr, x1_rows, x2_rows, ps):
    wg_v = W["w_branch_gate"].rearrange("(c p) f -> p c f", p=128)
    wb_v = [W[k].rearrange("(c p) f -> p c f", p=128) for k in ("w_branch_nsa", "w_branch_dn", "w_branch_mem")]
    wo_v = W["w_out"].rearrange("(c p) f -> p c f", p=128)
    ug = lambda tg: [uT.d(4 * tg + k) for k in range(4)]
    UID[0] += 1
    with contextlib.ExitStack() as esD:
        sbD = lambda name, shape, dt: Buf(esD.enter_context(nc.sbuf_tensor("D_%d_" % UID[0] + name, shape, dt)))
        wout = sbD("wout", [128, 8, D], BF16)
        for c in range(8):
            O.DMA("pool", wout[:, c, :], wo_v[:, c, :], [], [wout.d(c)])
        gpm = sbD("gpm", [128, D], F32)
        O.DMA("sp", gpm[:, :], W["mix_post_norm"].partition_broadcast(128), [], [gpm.d()])
        mT = sbD("mT", [128, 8, SEQ], BF16)
        wgd = [sbD("wgd%d" % k, [128, 8, 3, 128], BF16) for k in range(2)]
        wbd = [sbD("wbd%d" % k, [128, 3, 4, 128], BF16) for k in range(2)]
        sg = [sbD("sg%d" % k, [128, 512], F32) for k in range(2)]
        macc = sbD("macc", [128, 512], F32)
        tmpm = sbD("tmpm", [128, 512], F32)
        k_ = 0
        for dc in range(8):
            wg_ = wgd[dc % 2]
            wb_ = wbd[dc % 2]
            for br in range(3):
                O.DMA("pool", wg_[:, :, br, :], wg_v[:, :, br * 1024 + dc * 128:br * 1024 + dc * 128 + 128], [], [wg_.d(br)])
                O.DMA("pool", wb_[:, br, :, :], wb_v[br][:, :, dc * 128:(dc + 1) * 128], [], [wb_.d(br)])
            for tg in range(4):
                for br in range(3):
                    pG, pGd = PA.get("S")
                    for c in range(8):
                        O.MM(pG, wg_[:, c, br, :], uT[:, c, tg * 512:(tg + 1) * 512], c == 0, c == 7, [wg_.d(br)] + ug(tg), [pGd])
                    pY, pYd = PA.get("O")
                    od = [obr[br].d(4 * tg + k) for k in range(4)]
                    for kc in range(4):
                        O.MM(pY, wb_[:, br, kc, :], obr[br][:, kc, tg * 512:(tg + 1) * 512], kc == 0, kc == 3, [wb_.d(br)] + od, [pYd])
                    s_ = sg[k_ % 2]
                    k_ += 1
                    O.ACT(s_[:, :], pG, AF.Sigmoid, [pGd], [s_.d()])
                    if br == 0:
                        O.TT("dve", macc[:, :], pY, s_[:, :], ALU.mult, [pYd, s_.d()], [macc.d()])
                    else:
                        O.TT("dve", tmpm[:, :], pY, s_[:, :], ALU.mult, [pYd, s_.d()], [tmpm.d()])
                        if br == 1:
                            O.TT("dve", macc[:, :], macc[:, :], tmpm[:, :], ALU.add, [macc.d(), tmpm.d()], [macc.d()])
                        else:
                            O.TT("dve", mT[:, dc, tg * 512:(tg + 1) * 512], macc[:, :], tmpm[:, :], ALU.add, [macc.d(), tmpm.d()], [mT.d(tg)])
        xr = [sbD("xr%d" % k, [128, D], F32) for k in range(2)]
        ob = [sbD("ob%d" % k, [128, D], F32) for k in range(2)]
        st = [sbD("st%d" % k, [128, 8], F32) for k in range(2)]
        junk = sbD("junk", [128, D], BF16)
        for t in range(16):
            pd = ps[2] if t % 2 == 0 else ps[0]
            for hf in range(2):
                for dc in range(8):
                    O.MM(pd[:, hf * 512:(hf + 1) * 512], mT[:, dc, t * 128:(t + 1) * 128], wout[:, dc, hf * 512:(hf + 1) * 512], dc == 0, dc == 7,
                         [mT.d(t // 4), wout.d(dc)], [pd.d(hf)])
            s = st[t % 2]
            x_ = xr[t % 2]
            o_ = ob[t % 2]
            O.DMA("pool", x_[:, :], x1_rows[t * 128:(t + 1) * 128, :], [], [x_.d()])
            O.CP("act", o_[:, :], pd[:, :], [pd.d(0), pd.d(1)], [o_.d()])
            O.ACT(junk[:, :], o_[:, :], AF.Square, [o_.d()], [junk.d(), s.d()], accum=s[:, 0:1])
            rstd_ops(O, s, slice(0, 1), slice(1, 2), slice(2, 3), slice(3, 4), D)
            O.STT("dve", o_[:, :], o_[:, :], s[:, 3:4], gpm[:, :], ALU.mult, ALU.mult, [o_.d(), s.d(), gpm.d()], [o_.d()])
            O.TT("dve", o_[:, :], o_[:, :], x_[:, :], ALU.add, [o_.d(), x_.d()], [o_.d()])
            O.DMA("pool", x2_rows[t * 128:(t + 1) * 128, :], o_[:, :], [o_.d()], [], is_output=True)
        P.barrier()


def dn_phase(P, O, nc, PA, C, W, uT, odnT, winD_pre=None):
    ident = C["ident"]
    cf = C["cf"]
    cfd = [cf.d()]
    UTm, ONES, SLm, UPm = cf[:, 1, :], cf[:, 2, :], cf[:, 3, :], cf[:, 4, :]
    w_inv = W["w_in"].rearrange("(c p) f -> p c f", p=128)
    uall = [uT.d(k) for k in range(16)]
    ug = lambda tg: [uT.d(4 * tg + k) for k in range(4)]
    UID[0] += 1
    with contextlib.ExitStack() as esC:
        sbC = lambda name, shape, dt: Buf(esC.enter_context(nc.sbuf_tensor("C_%d_" % UID[0] + name, shape, dt)))
        if winD_pre is not None:
            winD = winD_pre
        else:
            winD = sbC("winD", [128, 8, 2056], BF16)
            for c in range(8):
                O.DMA("pool", winD[:, c, :], w_inv[:, c, 1304:3360], [], [winD.d(c)])
        wD = [winD.d(c) for c in range(8)]
        cw = sbC("cw", [128, 12, 4], F32)
        O.DMA("sp", cw[:, :, :], W["dn_conv_wT"], [], [cw.d()])
        sv = sbC("sv", [128, 16], F32)
        gdn = sbC("gdn", [128, 128], F32)
        O.DMA("sp", sv[:, 0:4], W["dn_a_log"].partition_broadcast(128), [], [sv.d()])
        O.DMA("sp", sv[:, 4:8], W["dn_dt_bias"].partition_broadcast(128), [], [sv.d()])
        O.DMA("sp", gdn[:, :], W["dn_out_norm"].partition_broadcast(128), [], [gdn.d()])
        onesb = sbC("onesb", [128, 128], BF16)
        O.MS("dve", onesb[:, :], 1.0, [onesb.d()])
        O.ACT(sv[:, 8:12], sv[:, 0:4], AF.Exp, [sv.d()], [sv.d()])
        O.TS("dve", sv[:, 12:16], sv[:, 8:12], -1.0, None, ALU.mult, None, [sv.d()], [sv.d()])
        ab = sbC("ab", [128, 16, 8], F32)
        for t in range(16):
            pb, pd = PA.get("M")
            for c in range(8):
                O.MM(pb[:, 0:8], uT[:, c, t * 128:(t + 1) * 128], winD[:, c, 1536:1544], c == 0, c == 7, [winD.d(c), uT.d(t)], [pd])
            O.CP("dve", ab[:, t, :], pb[:, 0:8], [pd], [ab.d()])
        gall = sbC("gall", [128, 16, 4], F32)
        beta = sbC("beta", [128, 16, 4], F32)
        nbeta = sbC("nbeta", [128, 16, 4], F32)
        O.TT("dve", gall[:, :, :], ab[:, :, 0:4], sv[:, 4:8].unsqueeze(1).to_broadcast([128, 16, 4]), ALU.add, [ab.d(), sv.d()], [gall.d()])
        O.ACT(gall[:, :, :], gall[:, :, :], AF.Exp, [gall.d()], [gall.d()])
        O.ACT(gall[:, :, :], gall[:, :, :], AF.Ln, [gall.d()], [gall.d()], bias=1.0)
        O.TT("dve", gall[:, :, :], gall[:, :, :], sv[:, 12:16].unsqueeze(1).to_broadcast([128, 16, 4]), ALU.mult, [gall.d(), sv.d()], [gall.d()])
        O.ACT(beta[:, :, :], ab[:, :, 4:8], AF.Sigmoid, [ab.d()], [beta.d()])
        O.TS("dve", nbeta[:, :, :], beta[:, :, :], -1.0, None, ALU.mult, None, [beta.d()], [nbeta.d()])
        qkv = [sbC("qcT", [128, SEQ], BF16), sbC("kT", [128, SEQ], BF16), sbC("vT", [128, SEQ], BF16)]
        qcT, kT, vT = qkv
        zs = sbC("zs", [128, 16, 128], BF16)
        S = sbC("S", [128, 128], F32)
        Sb = sbC("Sb", [128, 128], BF16)
        G_ = 4
        for h in range(4):
            with contextlib.ExitStack() as eh1:
                sb1 = lambda name, shape, dt: Buf(eh1.enter_context(nc.sbuf_tensor("C1_%d_%d_" % (UID[0], h) + name, shape, dt)))
                pre_l = [sb1("pre%d" % k, [128, 3 + SEQ], BF16) for k in range(2)]
                acc_l = [sb1("acc%d" % k, [128, SEQ], F32) for k in range(2)]
                sq_l = [sb1("sq%d" % k, [128, SEQ], BF16) for k in range(2)]
                rs = [sb1("rs%d" % k, [128, 512], F32) for k in range(2)]
                dg = sb1("dg", [128, 12, 128], BF16)
                for pre in pre_l:
                    O.MS("dve", pre[:, 0:3], 0.0, [pre.d(-1)])
                for which in range(3):
                    for i in range(4):
                        O.TS("dve", dg[:, which * 4 + i, :], cf[:, 0, :], cw[:, which * 4 + h, i:i + 1], None, ALU.mult, None, cfd + [cw.d()], [dg.d(which)])
                for which in range(3):
                    cc = which * 4 + h
                    col0 = cc * 128
                    pre, acc, sq = pre_l[which % 2], acc_l[which % 2], sq_l[which % 2]
                    for tg in range(4):
                        pb, pd = PA.get("M")
                        for c in range(8):
                            O.MM(pb, winD[:, c, col0:col0 + 128], uT[:, c, tg * 512:(tg + 1) * 512], c == 0, c == 7, [winD.d(c)] + ug(tg), [pd])
                        O.CP("act", pre[:, 3 + tg * 512:3 + (tg + 1) * 512], pb, [pd], [pre.d(tg)])
                    dst = qkv[which]
                    for tg in range(4):
                        sl = slice(tg * 512, (tg + 1) * 512)
                        pc, pcd = PA.get("M")
                        for i in range(4):
                            O.MM(pc, dg[:, which * 4 + i, :], pre[:, tg * 512 + i:tg * 512 + i + 512], i == 0, i == 3,
                                 [dg.d(which), pre.d(tg - 1), pre.d(tg)], [pcd])
                        if which == 2:
                            O.ACT(vT[:, sl], pc, AF.Silu, [pcd], [vT.d()])
                        else:
                            O.ACT(acc[:, sl], pc, AF.Silu, [pcd], [acc.d(tg)])
                            O.TT("dve", sq[:, sl], acc[:, sl], acc[:, sl], ALU.mult, [acc.d(tg)], [sq.d(tg)])
                    if which != 2:
                        for tg in range(4):
                            pb, pd = PA.get("M")
                            O.MM(pb, onesb[:, :], sq[:, tg * 512:(tg + 1) * 512], True, True, [onesb.d(), sq.d(tg)], [pd])
                            r_ = rs[tg % 2]
                            O.TS("dve", r_[:, :], pb, EPS, None, ALU.add, None, [pd], [r_.d()])
                            O.ACT(r_[:, :], r_[:, :], AF.Ln, [r_.d()], [r_.d()])
                            O.ACT(r_[:, :], r_[:, :], AF.Exp, [r_.d()], [r_.d()], scale=-0.5)
                            sl = slice(tg * 512, (tg + 1) * 512)
                            if which == 0:
                                O.STT("dve", dst[:, sl], acc[:, sl], 128 ** -0.5, r_[:, :], ALU.mult, ALU.mult, [acc.d(tg), r_.d()], [dst.d()])
                            else:
                                O.TT("dve", dst[:, sl], acc[:, sl], r_[:, :], ALU.mult, [acc.d(tg), r_.d()], [dst.d()])
                for t in range(16):
                    pb, pd = PA.get("M")
                    for c in range(8):
                        O.MM(pb[:, 0:128], uT[:, c, t * 128:(t + 1) * 128], winD[:, c, 1544 + h * 128:1544 + (h + 1) * 128], c == 0, c == 7,
                             [winD.d(c), uT.d(t)], [pd])
                    O.ACT(zs[:, t, :], pb[:, 0:128], AF.Silu, [pd], [zs.d()])
                P.barrier()
            with contextlib.ExitStack() as eh2:
                sb2 = lambda name, shape, dt: Buf(eh2.enter_context(nc.sbuf_tensor("C2_%d_%d_" % (UID[0], h) + name, shape, dt)))
                f32t = lambda nm, n_: [sb2("%s%d" % (nm, k), [128, 128], F32) for k in range(n_)]
                bft = lambda nm, n_: [sb2("%s%d" % (nm, k), [128, 128], BF16) for k in range(n_)]
                Gm, Dm, DTm, Er, t3, t4 = (f32t(n, G_) for n in ("Gm", "Dm", "DTm", "Er", "t3", "t4"))
                Pb = [[sb2("Pb%d_%d" % (k, j), [128, 256], F32) for j in range(2)] for k in range(G_)]
                yb = [[sb2("yb%d_%d" % (k, j), [128, 256], F32) for j in range(2)] for k in range(G_)]
                qdT, qkT, kdec, WT, Ub = (bft(n, 2 * G_) for n in ("qdT", "qkT", "kdec", "WT", "Ub"))
                cs = [sb2("cs%d" % k, [128, 12], F32) for k in range(2 * G_)]
                vnew, onb = (bft(n, 2) for n in ("vnew", "onb"))
                on_ = f32t("on", 2)
                junk = sb2("junk", [128, 128], BF16)
                O.MS("dve", S[:, :], 0.0, [S.d()])
                O.MS("dve", Sb[:, :], 0.0, [Sb.d()])
                ca = sb2("ca", [128, 6, 16], F32)
                pbm, pbmd = PA.get("M")
                O.MM(pbm[:, 0:16], UTm, gall[:, :, h], True, True, cfd + [gall.d()], [pbmd])
                O.MM(pbm[:, 16:32], ONES, gall[:, :, h], True, True, cfd + [gall.d()], [pbmd])
                O.CP("dve", ca[:, 0:2, :], pbm[:, 0:32].rearrange("p (a n) -> p a n", a=2), [pbmd], [ca.d()])
                O.ACT(ca[:, 2, :], ca[:, 0, :], AF.Exp, [ca.d()], [ca.d()])
                O.TT("dve", ca[:, 3, :], ca[:, 1, :], ca[:, 0, :], ALU.subtract, [ca.d()], [ca.d()])
                O.ACT(ca[:, 3, :], ca[:, 3, :], AF.Exp, [ca.d()], [ca.d()])
                O.ACT(ca[:, 4, :], ca[:, 1, :], AF.Exp, [ca.d()], [ca.d()])
                O.TT("dve", ca[:, 5, :], beta[:, :, h], ca[:, 2, :], ALU.mult, [beta.d(), ca.d()], [ca.d()])
                GmA = sb2("GmA", [128, 16, 128], F32)
                O.CP("dve", GmA[:, :, :], gall[:, :, h].unsqueeze(2).to_broadcast([128, 16, 128]), [gall.d()], [GmA.d()])

                def pre_gen(n):
                    k = n % G_
                    pk = n % (2 * G_)
                    tl = slice(n * 128, (n + 1) * 128)
                    c_ = cs[pk]
                    g_col = gall[:, n, h:h + 1]
                    b_col = beta[:, n, h:h + 1]
                    nb_col = nbeta[:, n, h:h + 1]
                    bk = PA.ps[k // 2]
                    bank = bk[:, (k % 2) * 512:(k % 2 + 1) * 512]
                    bd = bk.d(k % 2)
                    pK = bank[:, 0:256]
                    pR = bank[:, 256:384]
                    pb = bank[:, 256:258]
                    pv = bank.bitcast(BF16)[:, 768:1024]
                    O.MM(pK[:, 0:128], kT[:, tl], kT[:, tl], True, True, [kT.d()], [bd])
                    O.MM(pK[:, 128:256], kT[:, tl], qcT[:, tl], True, True, [kT.d(), qcT.d()], [bd])
                    O.TR(pv[:, 0:128], kT[:, tl], ident[:, :], [kT.d(), ident.d()], [bd])
                    O.TR(pv[:, 128:256], vT[:, tl], ident[:, :], [vT.d(), ident.d()], [bd])
                    O.MM(pR, GmA[:, n, :], UTm, True, True, cfd + [GmA.d()], [bd])
                    yield
                    O.TS("dve", Dm[k][:, :], pR, ca[:, 0, n:n + 1], 0.0, ALU.subtract, ALU.max, [bd, ca.d()], [Dm[k].d(), bd])
                    O.TS("dve", DTm[k][:, :], pR, ca[:, 0, n:n + 1], 0.0, ALU.subtract, ALU.min, [bd, ca.d()], [DTm[k].d(), bd])
                    O.ACT(Er[k][:, :], pR, AF.Exp, [bd], [Er[k].d(), bd])
                    yield
                    O.ACT(Dm[k][:, :], Dm[k][:, :], AF.Exp, [Dm[k].d()], [Dm[k].d()], scale=-1.0)
                    O.ACT(DTm[k][:, :], DTm[k][:, :], AF.Exp, [DTm[k].d()], [DTm[k].d()])
                    y0 = yb[k][0]
                    O.TS("dve", y0[:, 128:256], pv[:, 0:128], ca[:, 5, n:n + 1], None, ALU.mult, None, [bd, ca.d()], [y0.d(), bd])
                    O.TS("dve", kdec[pk][:, :], pv[:, 0:128], ca[:, 3, n:n + 1], None, ALU.mult, None, [bd, ca.d()], [kdec[pk].d(), bd])
                    O.TS("dve", y0[:, 0:128], pv[:, 128:256], b_col, None, ALU.mult, None, [bd, beta.d()], [y0.d(), bd])
                    O.TT("dve", qdT[pk][:, :], qcT[:, tl], Er[k][:, :], ALU.mult, [qcT.d(), Er[k].d()], [qdT[pk].d()])
                    yield
                    P0 = Pb[k][0]
                    O.STT("dve", t3[k][:, :], pK[:, 0:128], nb_col, Dm[k][:, :], ALU.mult, ALU.mult, [bd, nbeta.d(), Dm[k].d()], [t3[k].d(), bd])
                    O.TT("dve", t4[k][:, :], pK[:, 128:256], DTm[k][:, :], ALU.mult, [bd, DTm[k].d()], [t4[k].d(), bd])
                    yield
                    O.TT("dve", P0[:, 0:128], t3[k][:, :], SLm, ALU.mult, [t3[k].d()] + cfd, [P0.d()])
                    O.TT("dve", qkT[pk][:, :], t4[k][:, :], UPm, ALU.mult, [t4[k].d()] + cfd, [qkT[pk].d()])
                    yield
                    O.TR(bank[:, 0:128], P0[:, 0:128], cf[:, 0, :], [P0.d()] + cfd, [bd])
                    yield
                    O.CP("act", P0[:, 128:256], bank[:, 0:128], [bd], [P0.d(), bd])
                    yield
                    Pc = P0
                    y = y0
                    pa = bank[:, 0:256]
                    pq = bank[:, 256:512]
                    for l in range(7):
                        O.MM(pa, Pc[:, 128:256], y[:, :], True, True, [Pc.d(), y.d()], [bd])
                        if l < 6:
                            O.MM(pq[:, 0:128], Pc[:, 128:256], Pc[:, 0:128], True, True, [Pc.d()], [bd])
                            O.MM(pq[:, 128:256], Pc[:, 0:128], Pc[:, 128:256], True, True, [Pc.d()], [bd])
                        yield
                        yn = yb[k][(l + 1) % 2]
                        O.TT("dve", yn[:, :], y[:, :], pa, ALU.add, [y.d(), bd], [yn.d(), bd])
                        if l < 6:
                            Pn = Pb[k][(l + 1) % 2]
                            O.CP("act", Pn[:, :], pq, [bd], [Pn.d(), bd])
                            Pc = Pn
                        y = yn
                        yield
                    O.TR(bank[:, 0:128], y[:, 128:256], cf[:, 0, :], [y.d()] + cfd, [bd])
                    O.CP("act", Ub[pk][:, :], y[:, 0:128], [y.d()], [Ub[pk].d()])
                    yield
                    O.CP("act", WT[pk][:, :], bank[:, 0:128], [bd], [WT[pk].d(), bd])

                def scan_gen(ns):
                    p1, p1d = PA.ps[2][:, 0:512], PA.ps[2].d(0)
                    p2, p2d = PA.ps[2][:, 512:1024], PA.ps[2].d(1)
                    p3, p3d = PA.ps[3][:, 0:512], PA.ps[3].d(0)
                    pt4, pt4d = PA.ps[3][:, 512:1024], PA.ps[3].d(1)
                    for n in ns:
                        pk = n % (2 * G_)
                        k2 = n % 2
                        tl = slice(n * 128, (n + 1) * 128)
                        c_ = cs[pk]
                        O.MM(p1[:, 0:128], WT[pk][:, :], Sb[:, :], True, True, [WT[pk].d(), Sb.d()], [p1d])
                        yield
                        O.TT("dve", vnew[k2][:, :], Ub[pk][:, :], p1[:, 0:128], ALU.subtract, [Ub[pk].d(), p1d], [vnew[k2].d()])
                        yield
                        O.MM(p2[:, 0:128], qdT[pk][:, :], Sb[:, :], True, False, [qdT[pk].d(), Sb.d()], [p2d])
                        O.MM(p2[:, 0:128], qkT[pk][:, :], vnew[k2][:, :], False, True, [qkT[pk].d(), vnew[k2].d()], [p2d])
                        O.MM(p3[:, 0:128], kdec[pk][:, :], vnew[k2][:, :], True, True, [kdec[pk].d(), vnew[k2].d()], [p3d])
                        yield
                        O.STT("dve", S[:, :], S[:, :], ca[:, 4, n:n + 1], p3[:, 0:128], ALU.mult, ALU.add, [S.d(), ca.d(), p3d], [S.d()])
                        O.ACT(junk[:, :], p2[:, 0:128], AF.Square, [p2d], [junk.d(), c_.d()], accum=c_[:, 6:7])
                        yield
                        O.CP("act", Sb[:, :], S[:, :], [S.d()], [Sb.d()])
                        rstd_ops(O, c_, slice(6, 7), slice(7, 8), slice(8, 9), slice(9, 10), 128)
                        yield
                        O.STT("dve", on_[k2][:, :], p2[:, 0:128], c_[:, 9:10], gdn[:, :], ALU.mult, ALU.mult, [p2d, c_.d(), gdn.d()], [on_[k2].d()])
                        O.TT("dve", onb[k2][:, :], on_[k2][:, :], zs[:, n, :], ALU.mult, [on_[k2].d(), zs.d()], [onb[k2].d()])
                        yield
                        pv4 = pt4.bitcast(BF16)
                        O.TR(pv4[:, 0:128], onb[k2][:, :], ident[:, :], [onb[k2].d(), ident.d()], [pt4d])
                        yield
                        O.CP("act", odnT[:, h, tl], pv4[:, 0:128], [pt4d], [odnT.d(n)])

                def run_rr(gens):
                    gens = list(gens)
                    while gens:
                        for g_ in list(gens):
                            try:
                                next(g_)
                            except StopIteration:
                                gens.remove(g_)

                prev = None
                for g0 in range(0, 16, G_):
                    ns = list(range(g0, g0 + G_))
                    gens = [pre_gen(n) for n in ns]
                    if prev is not None:
                        gens.append(prev)
                    run_rr(gens)
                    prev = scan_gen(ns)
                run_rr([prev])
                P.barrier()
        P.barrier()


def _bucket(dist):
    dist = np.asarray(dist, dtype=np.int64)
    d = np.maximum(dist, 1).astype(np.float32)
    lb = 16 + (np.log(d / np.float32(16.0)).astype(np.float32) / np.float32(np.log(8.0)) * np.float32(16.0)).astype(np.int32)
    return np.where(dist < 16, dist, np.minimum(lb, 31)).astype(np.float32)


def host_consts():
    bf = ml_dtypes.bfloat16
    c = {}
    c["c_ident"] = np.eye(128, dtype=np.float32).astype(bf)
    ii = np.arange(128)
    cf = np.zeros((128, 5, 128), np.float32)
    cf[:, 0, :] = np.eye(128)
    cf[:, 1, :] = (ii[:, None] <= ii[None, :])
    cf[:, 2, :] = 1.0
    cf[:, 3, :] = (ii[:, None] > ii[None, :])
    cf[:, 4, :] = (ii[None, :] >= ii[:, None])
    c["c_f32"] = cf
    k = ii[:, None]
    q = ii[None, :]
    idx1 = np.zeros((128, 2, 128), np.float32)
    mask1 = np.zeros((128, 2, 128), np.float32)
    r0 = q - k
    idx1[:, 0, :] = _bucket(np.maximum(r0, 0))
    mask1[:, 0, :] = np.where(r0 >= 0, 0.0, -30000.0)
    idx1[:, 0, :] = np.where(r0 >= 0, idx1[:, 0, :], 31.0)
    idx1[:, 1, :] = _bucket(q - k + 128)
    c["c_idx1"] = idx1.reshape(128, 256)
    c["c_mask1"] = mask1.reshape(128, 256)
    cp = np.arange(247)[None, :] - 120
    dist = ii[:, None] - 16 * cp - 31
    c["c_idxc"] = np.where(dist >= 0, _bucket(np.maximum(dist, 0)), 31.0).astype(np.float32)
    c["c_maskc"] = np.where(dist >= 0, 0.0, -30000.0).astype(np.float32)
    E = np.zeros((32, 2048), np.float32)
    for n in range(32):
        E[n, n * 64:(n + 1) * 64] = 1.0
    c["c_E"] = E.astype(bf)
    c["c_band"] = np.where(k > q, 0.0, -30000.0).astype(np.float32).astype(bf)
    ub = np.zeros((128, 8, 32), np.float32)
    lb = np.zeros((128, 8, 32), np.float32)
    blk = np.arange(32)[None, :]
    for t in range(8):
        cur = ((8 + t) * 128 + ii) // 64
        cur = cur[:, None]
        causal = blk <= cur
        ub[:, t, :] = np.where(causal, 1e4, -1e4)
        l = np.full((128, 32), -1e4, np.float32)
        l = np.where(blk == cur - 1, 1e4, l)
        l = np.where(blk == cur, 2e4, l)
        l = np.where(blk == 0, 3e4, l)
        lb[:, t, :] = l
    c["c_ub"] = ub
    c["c_lb"] = lb
    c0 = np.arange(127) * 16
    s0 = np.arange(32) * 64
    ov = np.maximum(np.minimum(c0[:, None] + 32, s0[None, :] + 64) - np.maximum(c0[:, None], s0[None, :]), 0)
    ts = np.zeros((128, 32), np.float32)
    ts[:127] = ov.astype(np.float32) / 32.0
    c["c_tosel"] = ts.astype(bf)
    return c


WNAMES = {
    "w_in": [D, 3872], "cmp_k_w1": [2048, 256], "cmp_v_w1": [2048, 256], "cmp_k_w2": [256, 64], "cmp_v_w2": [256, 64],
    "cmp_pos_kT": [64, 32], "cmp_pos_vT": [64, 32], "dn_conv_wT": [128, 12, 4], "dn_a_log": [1, 4], "dn_dt_bias": [1, 4],
    "dn_out_norm": [1, 128], "mem_norm": [1, D], "w_mem_kv": [D, D], "w_branch_nsa": [512, D], "w_branch_dn": [512, D],
    "w_branch_mem": [512, D], "w_branch_gate": [D, 3 * D], "w_out": [D, D], "mix_post_norm": [1, D], "mix_pre_norm": [1, D],
}


SKIP = set()


def build_nc(nseq, dbg=False):
    nc = bass.Bass("TRN2", target_bir_lowering=False)
    ntok = nseq * SEQ
    dr = lambda name, shape, kind="ExternalInput", dt=F32: nc.dram_tensor(name, shape, dt, kind=kind).ap()
    x = dr("x", [ntok, D])
    mem = dr("mem", [nseq * 256, D])
    out = dr("out", [ntok, D], kind="ExternalOutput")
    X1 = out
    X2 = out
    if dbg:
        dbo = [dr("dbg_obr%d" % k, [128, 4, SEQ], kind="ExternalOutput", dt=BF16) for k in range(3)]
    F = {}
    for p in ("ffn1", "ffn2"):
        F[p] = dict(pre=dr(p + "_pre_norm", [1, D]), wg=dr(p + "_w_gate", [D, DFF]), wu=dr(p + "_w_up", [D, DFF]),
                    wd=dr(p + "_w_down", [DFF, D]), post=dr(p + "_post_norm", [1, D]))
    W = {k: dr(k, v) for k, v in WNAMES.items()}
    hc = host_consts()
    CD = {k: dr(k, list(v.shape), dt=(F32 if v.dtype == np.float32 else BF16)) for k, v in hc.items()}
    CD["rel_bias"] = dr("rel_bias", [1, 256])
    with contextlib.ExitStack() as es:
        P = Prog(nc, es)
        O = Ops(P)
        sb = lambda e_, name, shape, dt: Buf(e_.enter_context(nc.sbuf_tensor(name, shape, dt)))
        ident = sb(es, "ident", [128, 128], BF16)
        ps = [Buf(es.enter_context(nc.psum_tensor("ps%d" % i, [128, 1024], F32)), excl=True) for i in range(4)]
        PA = PsumAlloc(ps)
        O.DMA("sp", ident[:, :], CD["c_ident"], [], [ident.d()])
        G["nhalf"] = sb(es, "nhalf", [128, 8], F32)
        O.MS("pool", G["nhalf"][:, :], -0.5, [G["nhalf"].d()])
        with contextlib.ExitStack() as e1:
            f = F["ffn1"]
            ffn_phase(P, nc, e1, x, X1, f["wg"], f["wu"], f["wd"], f["pre"], f["post"], ident, ps, ntok, "f1_")
            P.barrier()
        with contextlib.ExitStack() as e2:
            C = setup_consts(P, O, nc, e2, CD, ident)
            uT = sb(e2, "uT", [128, 8, SEQ], BF16)
            obr = [sb(e2, "obrT%d" % k, [128, 4, SEQ], BF16) for k in range(3)]
            gmix = sb(e2, "gmix", [128, D], F32)
            O.DMA("sp", gmix[:, :], W["mix_pre_norm"].partition_broadcast(128), [], [gmix.d()])
            for s in range(nseq if "mixer" not in SKIP else 0):
                x1r = X1[s * SEQ:(s + 1) * SEQ, :]
                with contextlib.ExitStack() as e3:
                    tmp = {"xt": [sb(e3, "u%d_xt%d" % (s, i), [128, D], F32) for i in range(4)], "xs": [sb(e3, "u%d_xs%d" % (s, i), [128, D], BF16) for i in range(4)],
                           "st": [sb(e3, "u%d_st%d" % (s, i), [128, 8], F32) for i in range(4)], "junk": sb(e3, "u%d_junk" % s, [128, D], BF16)}
                    PA.mode = "dn"
                    for t in range(16):
                        norm_rows_T(P, O, nc, PA, C, tmp, x1r[t * 128:(t + 1) * 128, :], gmix, uT, t * 128, t)
                    PA.mode = "att"
                    P.barrier()
                if "nsa" not in SKIP:
                    nsa_phase(P, O, nc, PA, C, W, uT, obr[0], s)
                with contextlib.ExitStack() as e_pf:
                    winD = sb(e_pf, "winD_%d" % s, [128, 8, 2056], BF16)

                    def pf(winD=winD):
                        w_inv = W["w_in"].rearrange("(c p) f -> p c f", p=128)
                        for c in range(8):
                            O.DMA("pool", winD[:, c, :], w_inv[:, c, 1304:3360], [], [winD.d(c)])
                    mem_phase(P, O, nc, PA, C, W, uT, obr[2], mem[s * 256:(s + 1) * 256, :], after_weights=pf)
                    PA.mode = "dn"
                    dn_phase(P, O, nc, PA, C, W, uT, obr[1], winD_pre=winD)
                    PA.mode = "att"
                    P.barrier()
                if dbg and s == 0:
                    for k in range(3):
                        O.DMA("sp", dbo[k], obr[k][:, :, :], [obr[k].d(i) for i in range(16)], [], is_output=True)
                merge_phase(P, O, nc, PA, C, W, uT, obr, x1r, X2[s * SEQ:(s + 1) * SEQ, :], ps)
            P.barrier()
        with contextlib.ExitStack() as e4:
            f = F["ffn2"]
            ffn_phase(P, nc, e4, X2, out, f["wg"], f["wu"], f["wd"], f["pre"], f["post"], ident, ps, ntok, "f2_")
        P.barrier()
        P.finish()
        P.emit()
    return nc


def make_in_map(inputs, b0, nseq, consts):
    m = dict(consts)
    m["x"] = np.ascontiguousarray(inputs["x"][b0:b0 + nseq]).reshape(nseq * SEQ, D)
    m["mem"] = np.ascontiguousarray(inputs["mem"][b0:b0 + nseq]).reshape(nseq * 256, D)
    for p in ("ffn1", "ffn2"):
        m[p + "_pre_norm"] = np.ascontiguousarray(inputs[p + "_pre_norm"]).reshape(1, D)
        m[p + "_post_norm"] = np.ascontiguousarray(inputs[p + "_post_norm"]).reshape(1, D)
        for k in ("_w_gate", "_w_up", "_w_down"):
            m[p + k] = np.ascontiguousarray(inputs[p + k][0])
    for k in ("w_in", "cmp_k_w1", "cmp_v_w1", "cmp_k_w2", "cmp_v_w2", "w_mem_kv", "w_branch_nsa", "w_branch_dn", "w_branch_mem",
              "w_branch_gate", "w_out"):
        m[k] = np.ascontiguousarray(inputs[k][0])
    for k in ("dn_a_log", "dn_dt_bias", "dn_out_norm", "mem_norm", "mix_post_norm", "mix_pre_norm"):
        m[k] = np.ascontiguousarray(inputs[k]).reshape(1, -1)
    m["cmp_pos_kT"] = np.ascontiguousarray(inputs["cmp_pos_k"][0].T)
    m["cmp_pos_vT"] = np.ascontiguousarray(inputs["cmp_pos_v"][0].T)
    m["dn_conv_wT"] = np.ascontiguousarray(np.asarray(inputs["dn_conv_w"][0]).reshape(4, 12, 128).transpose(2, 1, 0))
    m["rel_bias"] = np.ascontiguousarray(inputs["rel_bias"]).reshape(1, 256)
    return m


def kernel(**inputs):
    inputs = {k: np.asarray(v, dtype=np.float32) for k, v in inputs.items()}
    n = 8
    nseq = 32 // n
    nc = build_nc(nseq)
    consts = host_consts()
    in_maps = [make_in_map(inputs, c * nseq, nseq, consts) for c in range(n)]
    res = run_bass_kernel_spmd(nc, in_maps, core_ids=list(range(n)))
    outs = [np.asarray(r["out"]).reshape(nseq, SEQ, D) for r in res.results]
    return np.concatenate(outs, axis=0).astype(np.float32)
```

```python
import contextlib
import numpy as np
import ml_dtypes
import concourse.bass as bass
import concourse.mybir as mybir
from concourse.bass_utils import run_bass_kernel_spmd

F32 = mybir.dt.float32
BF16 = mybir.dt.bfloat16
AF = mybir.ActivationFunctionType
ALU = mybir.AluOpType
AX = mybir.AxisListType

D = 1024
SEQ = 2048
DFF = 2816
NFF = DFF // 128
EPS = 1e-6
NDS = 40


class Dep:
    __slots__ = ("w", "r", "excl")

    def __init__(self, excl=False):
        self.w = None
        self.r = {}
        self.excl = excl


class Prog:
    def __init__(self, nc, es):
        self.nc = nc
        self.names = ["pe", "act", "dve", "pool", "sp"]
        self.sem = {n: es.enter_context(nc.semaphore("s_" + n)) for n in self.names}
        self.cnt = {n: 0 for n in self.names}
        self.q = {n: [] for n in self.names}
        self.seen = {n: {} for n in self.names}
        self.dsems = [es.enter_context(nc.semaphore("d%d" % i)) for i in range(NDS)]
        self.dval = [0] * NDS
        self.dnext = 0
        self.out_tokens = []

    def _waits(self, eng, reads, writes, is_dma):
        need = {}

        def add(t, same_ok):
            if t is None:
                return
            sem, val, src = t
            if same_ok and src == eng:
                return
            k = id(sem)
            if k not in need or need[k][1] < val:
                need[k] = (sem, val)

        for d in reads:
            add(d.w, False)
        for d in writes:
            add(d.w, not is_dma)
            for t in d.r.values():
                add(t, not is_dma)
        out = []
        for k, (sem, val) in need.items():
            if self.seen[eng].get(k, 0) >= val:
                continue
            self.seen[eng][k] = val
            out.append((sem, val))
        return out

    def op(self, eng, fn, reads=(), writes=()):
        rx = [d for d in reads if d.excl and d not in writes]
        if rx:
            writes = list(writes) + rx
        waits = self._waits(eng, reads, writes, False)
        self.cnt[eng] += 1
        tok = (self.sem[eng], self.cnt[eng], eng)
        self.q[eng].append((waits, fn, self.sem[eng], 1))
        for d in reads:
            d.r[id(tok[0])] = tok
        for d in writes:
            d.w = tok
            d.r = {}
        return tok

    def dma(self, eng, fn, reads=(), writes=(), is_output=False):
        i = self.dnext
        self.dnext = (i + 1) % NDS
        waits = self._waits(eng, reads, writes, True)
        ds = self.dsems[i]
        if self.dval[i] > 0 and self.seen[eng].get(id(ds), 0) < self.dval[i]:
            waits.append((ds, self.dval[i]))
            self.seen[eng][id(ds)] = self.dval[i]
        self.dval[i] += 16
        tok = (ds, self.dval[i], "dma")
        self.q[eng].append((waits, fn, ds, 16))
        for d in reads:
            d.r[id(ds)] = tok
        for d in writes:
            d.w = tok
            d.r = {}
        if is_output:
            self.out_tokens.append(tok)
        return tok

    def barrier(self):
        for e in self.names:
            waits = []
            for f in self.names:
                if f == e or self.cnt[f] == 0:
                    continue
                k = id(self.sem[f])
                if self.seen[e].get(k, 0) < self.cnt[f]:
                    self.seen[e][k] = self.cnt[f]
                    waits.append((self.sem[f], self.cnt[f]))
            for i in range(NDS):
                k = id(self.dsems[i])
                if self.dval[i] > 0 and self.seen[e].get(k, 0) < self.dval[i]:
                    self.seen[e][k] = self.dval[i]
                    waits.append((self.dsems[i], self.dval[i]))
            if waits:
                self.q[e].append((waits, None, None, 0))

    def finish(self):
        waits = {}
        for sem, val, _ in self.out_tokens:
            k = id(sem)
            if k not in waits or waits[k][1] < val:
                waits[k] = (sem, val)
        self.q["sp"].append((list(waits.values()), None, None, 0))

    def emit(self):
        nc = self.nc
        with nc.Block() as block:

            def replay(e, name):
                for waits, fn, sem, inc in self.q[name]:
                    if fn is None:
                        for s, v in waits:
                            e.wait_ge(s, v)
                        continue
                    for s, v in waits:
                        e.wait_ge(s, v)
                    fn(e).then_inc(sem, inc)

            @block.sync
            def _(e):
                replay(e, "sp")

            @block.tensor
            def _(e):
                replay(e, "pe")

            @block.scalar
            def _(e):
                replay(e, "act")

            @block.vector
            def _(e):
                replay(e, "dve")

            @block.gpsimd
            def _(e):
                replay(e, "pool")


class Buf:
    def __init__(self, t, excl=False):
        self.t = t
        self.deps = {}
        self.excl = excl

    def d(self, key=0):
        if key not in self.deps:
            self.deps[key] = Dep(self.excl)
        return self.deps[key]

    def all(self):
        return list(self.deps.values())

    def __getitem__(self, k):
        return self.t[k]


def dram_rows_bcast(ap2d_row, nparts):
    return ap2d_row.partition_broadcast(nparts)


def ffn_phase(P, nc, es, x_src, x_dst, w_gate, w_up, w_down, g_pre, g_post, ident, ps, ntok, tagp):
    sb = lambda name, shape, dt: Buf(es.enter_context(nc.sbuf_tensor(tagp + name, shape, dt)))
    wg = sb("wg", [128, 8, DFF], BF16)
    wu = sb("wu", [128, 8, DFF], BF16)
    wd = sb("wd", [128, NFF, D], BF16)
    gpre = sb("gpre", [128, D], F32)
    gpost = sb("gpost", [128, D], F32)
    xnT = [sb("xnT%d" % i, [128, 8, 512], BF16) for i in range(2)]
    actT = sb("actT", [128, NFF, 512], BF16)
    xt = [sb("xt%d" % i, [128, D], F32) for i in range(2)]
    xs = [sb("xs%d" % i, [128, D], BF16) for i in range(2)]
    sg = [sb("sg%d" % i, [128, 512], BF16) for i in range(2)]
    ob = [sb("ob%d" % i, [128, D], F32) for i in range(2)]
    xr = [sb("xr%d" % i, [128, D], F32) for i in range(1)]
    junk = sb("junk", [128, D], BF16)
    st = [sb("st%d" % i, [128, 8], F32) for i in range(4)]

    wgv = w_gate.rearrange("(c p) f -> p c f", p=128)
    wuv = w_up.rearrange("(c p) f -> p c f", p=128)
    wdv = w_down.rearrange("(c p) f -> p c f", p=128)
    P.dma("sp", lambda e: e.dma_start(out=gpre[:, :], in_=g_pre.partition_broadcast(128)), writes=[gpre.d()])
    P.dma("sp", lambda e: e.dma_start(out=gpost[:, :], in_=g_post.partition_broadcast(128)), writes=[gpost.d()])
    for c in range(8):
        P.dma("pool", lambda e, c=c: e.dma_start(out=wg[:, c, :], in_=wgv[:, c, :]), writes=[wg.d(c)])
        P.dma("pool", lambda e, c=c: e.dma_start(out=wu[:, c, :], in_=wuv[:, c, :]), writes=[wu.d(c)])
    for f in range(NFF):
        P.dma("pool", lambda e, f=f: e.dma_start(out=wd[:, f, :], in_=wdv[:, f, :]), writes=[wd.d(f)])

    ngrp = ntok // 512
    nst = [0]

    def prep_group(g):
        xb = xnT[g % 2]
        for tt in range(4):
            r0 = g * 512 + tt * 128
            i = (g * 4 + tt) % 2
            s = st[(g * 4 + tt) % 4]
            P.dma("sp", lambda e, i=i, r0=r0: e.dma_start(out=xt[i][:, :], in_=x_src[r0:r0 + 128, :]), writes=[xt[i].d()])
            P.op("act", lambda e, i=i, s=s: e.activation(out=junk[:, :], in_=xt[i][:, :], func=AF.Square, accum_out=s[:, 0:1]),
                 reads=[xt[i].d()], writes=[junk.d(), s.d()])
            P.op("dve", lambda e, s=s: e.tensor_scalar(out=s[:, 1:2], in0=s[:, 0:1], scalar1=1.0 / D, scalar2=EPS, op0=ALU.mult, op1=ALU.add),
                 reads=[s.d()], writes=[s.d()])
            P.op("pool", lambda e, s=s: e.tensor_tensor(out=s[:, 3:4], in0=s[:, 1:2], in1=G["nhalf"][:, 0:1], op=ALU.pow),
                 reads=[s.d(), G["nhalf"].d()], writes=[s.d()])
            P.op("dve", lambda e, i=i, s=s: e.scalar_tensor_tensor(out=xs[i][:, :], in0=xt[i][:, :], scalar=s[:, 3:4], in1=gpre[:, :], op0=ALU.mult, op1=ALU.mult),
                 reads=[xt[i].d(), s.d(), gpre.d()], writes=[xs[i].d()])
            pb = ps[3]
            pv = pb[:, 0:512].bitcast(BF16)
            for c in range(8):
                P.op("pe", lambda e, i=i, c=c, pv=pv: e.transpose(out=pv[:, c * 128:(c + 1) * 128], in_=xs[i][:, c * 128:(c + 1) * 128], identity=ident[:, :]),
                     reads=[xs[i].d(), ident.d()], writes=[pb.d(0)])
            P.op("dve", lambda e, xb=xb, tt=tt, pv=pv: e.tensor_copy(out=xb[:, :, tt * 128:(tt + 1) * 128], in_=pv.rearrange("p (c t) -> p c t", c=8)),
                 reads=[pb.d(0)], writes=[xb.d()])

    def gate_up(g):
        xb = xnT[g % 2]
        for f in range(NFF):
            pg = ps[0] if f % 2 == 0 else ps[1]
            for c in range(8):
                P.op("pe", lambda e, pg=pg, c=c, f=f, xb=xb: e.matmul(pg[:, 0:512], lhsT=wg[:, c, f * 128:(f + 1) * 128], rhs=xb[:, c, :], start=(c == 0), stop=(c == 7)),
                     reads=[wg.d(c), xb.d()], writes=[pg.d(0)])
            for c in range(8):
                P.op("pe", lambda e, pg=pg, c=c, f=f, xb=xb: e.matmul(pg[:, 512:1024], lhsT=wu[:, c, f * 128:(f + 1) * 128], rhs=xb[:, c, :], start=(c == 0), stop=(c == 7)),
                     reads=[wu.d(c), xb.d()], writes=[pg.d(1)])
            sgb = sg[f % 2]
            P.op("act", lambda e, pg=pg, sgb=sgb: e.activation(out=sgb[:, :], in_=pg[:, 0:512], func=AF.Silu), reads=[pg.d(0)], writes=[sgb.d()])
            P.op("dve", lambda e, pg=pg, sgb=sgb, f=f: e.tensor_tensor(out=actT[:, f, :], in0=pg[:, 512:1024], in1=sgb[:, :], op=ALU.mult),
                 reads=[pg.d(1), sgb.d()], writes=[actT.d(f)])

    def down(g):
        for tt in range(4):
            r0 = g * 512 + tt * 128
            pd = ps[2] if tt % 2 == 0 else ps[0]
            for h in range(2):
                for f in range(NFF):
                    P.op("pe", lambda e, pd=pd, h=h, f=f, tt=tt: e.matmul(pd[:, h * 512:(h + 1) * 512], lhsT=actT[:, f, tt * 128:(tt + 1) * 128], rhs=wd[:, f, h * 512:(h + 1) * 512], start=(f == 0), stop=(f == NFF - 1)),
                         reads=[actT.d(f), wd.d(f)], writes=[pd.d(h)])
            s = st[(g * 4 + tt) % 4]
            o = ob[(g * 4 + tt) % 2]
            P.dma("sp", lambda e, r0=r0: e.dma_start(out=xr[0][:, :], in_=x_src[r0:r0 + 128, :]), writes=[xr[0].d()])
            P.op("act", lambda e, pd=pd, o=o: e.copy(out=o[:, :], in_=pd[:, :]), reads=[pd.d(0), pd.d(1)], writes=[o.d()])
            P.op("act", lambda e, o=o, s=s: e.activation(out=junk[:, :], in_=o[:, :], func=AF.Square, accum_out=s[:, 4:5]),
                 reads=[o.d()], writes=[junk.d(), s.d()])
            P.op("dve", lambda e, s=s: e.tensor_scalar(out=s[:, 5:6], in0=s[:, 4:5], scalar1=1.0 / D, scalar2=EPS, op0=ALU.mult, op1=ALU.add),
                 reads=[s.d()], writes=[s.d()])
            P.op("pool", lambda e, s=s: e.tensor_tensor(out=s[:, 7:8], in0=s[:, 5:6], in1=G["nhalf"][:, 0:1], op=ALU.pow),
                 reads=[s.d(), G["nhalf"].d()], writes=[s.d()])
            P.op("dve", lambda e, s=s, o=o: e.scalar_tensor_tensor(out=o[:, :], in0=o[:, :], scalar=s[:, 7:8], in1=gpost[:, :], op0=ALU.mult, op1=ALU.mult),
                 reads=[o.d(), s.d(), gpost.d()], writes=[o.d()])
            P.op("dve", lambda e, o=o: e.scalar_tensor_tensor(out=o[:, :], in0=o[:, :], scalar=0.5, in1=xr[0][:, :], op0=ALU.mult, op1=ALU.add),
                 reads=[o.d(), xr[0].d()], writes=[o.d()])
            P.dma("pool", lambda e, o=o, r0=r0: e.dma_start(out=x_dst[r0:r0 + 128, :], in_=o[:, :]), reads=[o.d()], writes=[], is_output=True)

    prep_group(0)
    for g in range(ngrp):
        gate_up(g)
        if g + 1 < ngrp:
            prep_group(g + 1)
        down(g)


UID = [0]


class Ops:
    def __init__(self, P):
        self.P = P

    def MM(self, out, lhsT, rhs, st, sp, r, w):
        self.P.op("pe", lambda e: e.matmul(out, lhsT=lhsT, rhs=rhs, start=st, stop=sp), r, w)

    def TR(self, out, in_, ident, r, w):
        self.P.op("pe", lambda e: e.transpose(out=out, in_=in_, identity=ident), r, w)

    def ACT(self, out, in_, func, r, w, bias=None, scale=None, accum=None):
        kw = {}
        if bias is not None:
            kw["bias"] = bias
        if scale is not None:
            kw["scale"] = scale
        if accum is not None:
            kw["accum_out"] = accum
        self.P.op("act", lambda e: e.activation(out=out, in_=in_, func=func, **kw), r, w)

    def TT(self, eng, out, in0, in1, op, r, w):
        self.P.op(eng, lambda e: e.tensor_tensor(out=out, in0=in0, in1=in1, op=op), r, w)

    def TS(self, eng, out, in0, s1, s2, op0, op1, r, w):
        if s2 is None:
            self.P.op(eng, lambda e: e.tensor_scalar(out=out, in0=in0, scalar1=s1, scalar2=None, op0=op0), r, w)
        else:
            self.P.op(eng, lambda e: e.tensor_scalar(out=out, in0=in0, scalar1=s1, scalar2=s2, op0=op0, op1=op1), r, w)

    def STT(self, eng, out, in0, scalar, in1, op0, op1, r, w):
        self.P.op(eng, lambda e: e.scalar_tensor_tensor(out=out, in0=in0, scalar=scalar, in1=in1, op0=op0, op1=op1), r, w)

    def CP(self, eng, out, in_, r, w):
        if eng == "act":
            self.P.op(eng, lambda e: e.copy(out=out, in_=in_), r, w)
        else:
            self.P.op(eng, lambda e: e.tensor_copy(out=out, in_=in_), r, w)

    def MS(self, eng, ap, val, w):
        self.P.op(eng, lambda e: e.memset(ap, val), [], w)

    def RCP(self, out, in_, r, w):
        self.P.op("dve", lambda e: e.reciprocal(out=out, in_=in_), r, w)

    def DMA(self, q, out, in_, r, w, is_output=False):
        self.P.dma(q, lambda e: e.dma_start(out=out, in_=in_), r, w, is_output=is_output)


G = {}


def rstd_ops(O, s, i_ss, i_tmp, i_sq, i_out, n, r_extra=()):
    nh = G["nhalf"]
    O.TS("dve", s[:, i_tmp], s[:, i_ss], 1.0 / n, EPS, ALU.mult, ALU.add, [s.d()], [s.d()])
    O.TT("pool", s[:, i_out], s[:, i_tmp], nh[:, 0:1], ALU.pow, [s.d(), nh.d()], [s.d()])


def setup_consts(P, O, nc, es, CD, ident):
    sbp = lambda name, shape, dt: Buf(es.enter_context(nc.sbuf_tensor(name, shape, dt)))
    C = {"ident": ident}
    C["cf"] = cf = sbp("cf", [128, 5, 128], F32)
    C["T1d"] = T1d = sbp("T1d", [128, 8, 256], BF16)
    C["Gc"] = Gc = sbp("Gc", [128, 8, 247], BF16)
    C["Eb"] = Eb = sbp("Eb", [32, 2048], BF16)
    C["band"] = band = sbp("band", [128, 128], BF16)
    C["ub"] = ub = sbp("ub", [128, 8, 32], F32)
    C["lb"] = lb = sbp("lb", [128, 8, 32], F32)
    C["tab"] = tab = sbp("tab", [128, 256], F32)
    C["tosel"] = tosel = sbp("tosel", [128, 32], BF16)
    O.DMA("sp", cf[:, :, :], CD["c_f32"], [], [cf.d()])
    O.DMA("sp", Eb[:, :], CD["c_E"], [], [Eb.d()])
    O.DMA("sp", band[:, :], CD["c_band"], [], [band.d()])
    O.DMA("sp", ub[:, :, :], CD["c_ub"], [], [ub.d()])
    O.DMA("sp", lb[:, :, :], CD["c_lb"], [], [lb.d()])
    O.DMA("sp", tosel[:, :], CD["c_tosel"], [], [tosel.d()])
    O.DMA("sp", tab[:, :], CD["rel_bias"].partition_broadcast(128), [], [tab.d()])
    with contextlib.ExitStack() as et:
        sbt = lambda name, shape, dt: Buf(et.enter_context(nc.sbuf_tensor(name, shape, dt)))
        idx1 = sbt("idx1", [128, 256], F32)
        mask1 = sbt("mask1", [128, 256], F32)
        idxc = sbt("idxc", [128, 247], F32)
        maskc = sbt("maskc", [128, 247], F32)
        tabd = sbt("tabd", [128, 256], F32)
        oh1 = sbt("oh1", [128, 256], F32)
        ohc = sbt("ohc", [128, 247], F32)
        acc1 = sbt("acc1", [128, 8, 256], F32)
        accc = sbt("accc", [128, 8, 247], F32)
        O.DMA("sp", idx1[:, :], CD["c_idx1"], [], [idx1.d()])
        O.DMA("sp", mask1[:, :], CD["c_mask1"], [], [mask1.d()])
        O.DMA("sp", idxc[:, :], CD["c_idxc"], [], [idxc.d()])
        O.DMA("sp", maskc[:, :], CD["c_maskc"], [], [maskc.d()])
        t3 = tab[:, :].rearrange("p (b h) -> p b h", h=8)
        O.TT("dve", tabd[:, :].rearrange("p (b h) -> p b h", h=8), t3, tab[:, 248:256].unsqueeze(1).to_broadcast([128, 32, 8]), ALU.subtract,
             [tab.d()], [tabd.d()])
        for h in range(8):
            O.CP("dve", acc1[:, h, :], mask1[:, :], [mask1.d()], [acc1.d()])
            O.CP("dve", accc[:, h, :], maskc[:, :], [maskc.d()], [accc.d()])
        for b in range(31):
            O.TS("dve", oh1[:, :], idx1[:, :], float(b), None, ALU.is_equal, None, [idx1.d()], [oh1.d()])
            O.TS("dve", ohc[:, :], idxc[:, :], float(b), None, ALU.is_equal, None, [idxc.d()], [ohc.d()])
            for h in range(8):
                O.STT("dve", acc1[:, h, :], oh1[:, :], tabd[:, b * 8 + h:b * 8 + h + 1], acc1[:, h, :], ALU.mult, ALU.add,
                      [oh1.d(), tabd.d(), acc1.d()], [acc1.d()])
                O.STT("dve", accc[:, h, :], ohc[:, :], tabd[:, b * 8 + h:b * 8 + h + 1], accc[:, h, :], ALU.mult, ALU.add,
                      [ohc.d(), tabd.d(), accc.d()], [accc.d()])
        O.CP("dve", T1d[:, :, :], acc1[:, :, :], [acc1.d()], [T1d.d()])
        O.CP("dve", Gc[:, :, :], accc[:, :, :], [accc.d()], [Gc.d()])
        P.barrier()
    return C


class PsumAlloc:
    def __init__(self, ps):
        self.ps = ps
        self.ctr = {}
        self.mode = "att"
        self.maps = {"att": {"S": [(0, 0), (0, 1), (2, 0)], "O": [(1, 0), (1, 1)], "T": [(2, 1)], "M": [(3, 0), (3, 1)]},
                     "att2": {"S": [(0, 0), (0, 1), (2, 0), (3, 0)], "O": [(1, 0), (1, 1)], "T": [(2, 1)], "M": [(3, 1)]},
                     "dn": {"S": [(0, 0), (0, 1)], "O": [(1, 0), (1, 1)], "T": [(2, 0), (2, 1)], "M": [(3, 0), (3, 1)]}}

    def get(self, pool):
        banks = self.maps[self.mode][pool]
        k = self.ctr.get(pool, 0)
        self.ctr[pool] = k + 1
        b, h = banks[k % len(banks)]
        buf = self.ps[b]
        return buf[:, h * 512:(h + 1) * 512], buf.d(h)


def norm_rows_T(P, O, nc, PA, C, tmp, src_rows_ap, gain, dstT, col0, k):
    xt = tmp["xt"][k % len(tmp["xt"])]
    xs = tmp["xs"][k % len(tmp["xs"])]
    s = tmp["st"][k % 4]
    junk = tmp["junk"]
    ident = C["ident"]
    O.DMA("sp", xt[:, :], src_rows_ap, [], [xt.d()])
    O.ACT(junk[:, :], xt[:, :], AF.Square, [xt.d()], [junk.d(), s.d()], accum=s[:, 0:1])
    rstd_ops(O, s, slice(0, 1), slice(1, 2), slice(2, 3), slice(3, 4), D)
    O.STT("dve", xs[:, :], xt[:, :], s[:, 3:4], gain[:, :], ALU.mult, ALU.mult, [xt.d(), s.d(), gain.d()], [xs.d()])
    pb, pd = PA.get("T")
    pv = pb.bitcast(BF16)
    for c in range(8):
        O.TR(pv[:, c * 128:(c + 1) * 128], xs[:, c * 128:(c + 1) * 128], ident[:, :], [xs.d(), ident.d()], [pd])
    O.CP("dve", dstT[:, :, col0:col0 + 128], pv.rearrange("p (c t) -> p c t", c=8), [pd], [dstT.d(col0 // 128)])


def nsa_phase(P, O, nc, PA, C, W, uT, onsaT, s, dbg=None):
    ident, T1d, Gc, Eb, band, ub, lb, tab, tosel = (C[k] for k in ("ident", "T1d", "Gc", "Eb", "band", "ub", "lb", "tab", "tosel"))
    w_inv = W["w_in"].rearrange("(c p) f -> p c f", p=128)
    uall = [uT.d(k) for k in range(16)]
    ug = lambda tg: [uT.d(4 * tg + k) for k in range(4)]
    UID[0] += 1
    with contextlib.ExitStack() as esA:
        sbA = lambda name, shape, dt: Buf(esA.enter_context(nc.sbuf_tensor("A_%d_" % UID[0] + name, shape, dt)))
        winA = sbA("winA", [128, 8, 1304], BF16)
        for c in range(8):
            O.DMA("pool", winA[:, c, :], w_inv[:, c, 0:1304], [], [winA.d(c)])
        wA = [winA.d(c) for c in range(8)]
        kcmpT = sbA("kcmpT", [65, 2, 128], BF16)
        VcX = sbA("VcX", [128, 2, 97], BF16)
        with contextlib.ExitStack() as esA1:
            sb1 = lambda name, shape, dt: Buf(esA1.enter_context(nc.sbuf_tensor("A1_%d_" % UID[0] + name, shape, dt)))
            kvT = [sb1("kcT", [64, 2, 2048], BF16), sb1("vcT", [64, 2, 2048], BF16)]
            w1 = [sb1("w1k", [64, 32, 256], BF16), sb1("w1v", [64, 32, 256], BF16)]
            w2 = [sb1("w2k", [128, 2, 64], BF16), sb1("w2v", [128, 2, 64], BF16)]
            pos = [sb1("posk", [64, 32], BF16), sb1("posv", [64, 32], BF16)]
            hb = sb1("hb", [128, 4], F32)
            hT = sb1("hT", [128, 8, 127], BF16)
            for kind, nm in enumerate(("k", "v")):
                O.DMA("pool", w1[kind][:, :, :], W["cmp_%s_w1" % nm].rearrange("(j d) h -> d j h", d=64), [], [w1[kind].d()])
                O.DMA("pool", w2[kind][:, :, :], W["cmp_%s_w2" % nm].rearrange("(c p) f -> p c f", p=128), [], [w2[kind].d()])
                O.DMA("pool", pos[kind][:, :], W["cmp_pos_%sT" % nm], [], [pos[kind].d()])
            for kind in range(2):
                col0 = 512 + kind * 128
                for g in range(2):
                    for tg in range(4):
                        pb, pd = PA.get("M")
                        for c in range(8):
                            O.MM(pb[0:64, :], winA[:, c, col0 + g * 64:col0 + g * 64 + 64], uT[:, c, tg * 512:(tg + 1) * 512], c == 0, c == 7,
                                 [winA.d(c)] + ug(tg), [pd])
                        O.CP("act", kvT[kind][0:64, g, tg * 512:(tg + 1) * 512], pb[0:64, :], [pd], [kvT[kind].d()])
            for kind in range(2):
                for ch in range(2):
                    pb, pd = PA.get("M")
                    for j in range(32):
                        O.MM(pb[:, 0:1], w1[kind][0:64, j, ch * 128:(ch + 1) * 128], pos[kind][0:64, j:j + 1], j == 0, j == 31,
                             [w1[kind].d(), pos[kind].d()], [pd])
                    O.CP("dve", hb[:, kind * 2 + ch:kind * 2 + ch + 1], pb[:, 0:1], [pd], [hb.d()])
            for kind in range(2):
                for g in range(2):
                    for ch in range(2):
                        pb, pd = PA.get("M")
                        for j in range(32):
                            O.MM(pb[:, 0:127], w1[kind][0:64, j, ch * 128:(ch + 1) * 128], kvT[kind][0:64, g, j:j + 2017:16], j == 0, j == 31,
                                 [w1[kind].d(), kvT[kind].d()], [pd])
                        O.ACT(hT[:, kind * 4 + g * 2 + ch, :], pb[:, 0:127], AF.Silu, [pd, hb.d()], [hT.d()],
                              bias=hb[:, kind * 2 + ch:kind * 2 + ch + 1])
            for g in range(2):
                pb, pd = PA.get("M")
                for ch in range(2):
                    O.MM(pb[0:64, 0:127], w2[0][:, ch, :], hT[:, g * 2 + ch, :], ch == 0, ch == 1, [w2[0].d(), hT.d()], [pd])
                O.CP("dve", kcmpT[0:64, g, 0:127], pb[0:64, 0:127], [pd], [kcmpT.d()])
                pb, pd = PA.get("M")
                for ch in range(2):
                    O.MM(pb[0:127, 0:64], hT[:, 4 + g * 2 + ch, :], w2[1][:, ch, :], ch == 0, ch == 1, [w2[1].d(), hT.d()], [pd])
                O.CP("dve", VcX[0:127, g, 0:64], pb[0:127, 0:64], [pd], [VcX.d()])
                O.CP("dve", VcX[0:127, g, 65:97], tosel[0:127, :], [tosel.d()], [VcX.d()])
            O.MS("dve", kcmpT[64:65, :, :], 1.0, [kcmpT.d()])
            O.MS("dve", VcX[:, :, 64:65], 1.0, [VcX.d()])
            P.barrier()
        qT = sbA("qT", [65, 8, 2048], BF16)
        ksT = sbA("ksT", [65, 2, 2048], BF16)
        kwT = sbA("kwT", [65, 2, 2048], BF16)
        Vs1 = sbA("Vs1", [128, 16, 2, 65], BF16)
        Vw1 = sbA("Vw1", [128, 16, 2, 65], BF16)
        gs = sbA("gs", [128, 16, 24], F32)
        for h in range(8):
            for tg in range(4):
                pb, pd = PA.get("M")
                for c in range(8):
                    O.MM(pb[0:64, :], winA[:, c, h * 64:(h + 1) * 64], uT[:, c, tg * 512:(tg + 1) * 512], c == 0, c == 7, [winA.d(c)] + ug(tg), [pd])
                P.op("act", lambda e, h=h, tg=tg, pb=pb: e.mul(out=qT[0:64, h, tg * 512:(tg + 1) * 512], in_=pb[0:64, :], mul=0.125), [pd], [qT.d(tg)])
            O.TS("dve", qT[64:65, h, :], uT[64:65, 0, :], 0.0, tab[64:65, 248 + h:249 + h], ALU.mult, ALU.add, uall + [tab.d()],
                 [qT.d(k) for k in range(4)])
        for dst, col0 in ((ksT, 768), (kwT, 1024)):
            for g in range(2):
                for tg in range(4):
                    pb, pd = PA.get("M")
                    for c in range(8):
                        O.MM(pb[0:64, :], winA[:, c, col0 + g * 64:col0 + g * 64 + 64], uT[:, c, tg * 512:(tg + 1) * 512], c == 0, c == 7,
                             [winA.d(c)] + ug(tg), [pd])
                    O.CP("act", dst[0:64, g, tg * 512:(tg + 1) * 512], pb[0:64, :], [pd], [dst.d()])
            O.MS("dve", dst[64:65, :, :], 1.0, [dst.d()])
        for t in range(16):
            pb, pd = PA.get("M")
            for c in range(8):
                O.MM(pb[:, 0:408], uT[:, c, t * 128:(t + 1) * 128], winA[:, c, 896:1304], c == 0, c == 7, [winA.d(c), uT.d(t)], [pd])
            O.CP("dve", Vs1[:, t, :, 0:64], pb[:, 0:128].rearrange("p (g d) -> p g d", g=2), [pd], [Vs1.d()])
            O.CP("dve", Vw1[:, t, :, 0:64], pb[:, 256:384].rearrange("p (g d) -> p g d", g=2), [pd], [Vw1.d()])
            O.ACT(gs[:, t, :], pb[:, 384:408], AF.Sigmoid, [pd], [gs.d()])
        O.MS("dve", Vs1[:, :, :, 64:65], 1.0, [Vs1.d()])
        O.MS("dve", Vw1[:, :, :, 64:65], 1.0, [Vw1.d()])
        PT = [sbA("PT%d" % k, [128, 512], BF16) for k in range(5)]
        onsa = [sbA("onsa%d" % k, [128, 8, 64], F32) for k in range(2)]
        onsab = [sbA("onsab%d" % k, [128, 512], BF16) for k in range(2)]
        sm = [sbA("sm%d" % k, [128, 16], F32) for k in range(4)]
        tmpo = [sbA("tmpo%d" % k, [128, 4, 64], F32) for k in range(2)]
        timp = sbA("timp", [128, 4, 32], F32)
        sc = [sbA("sc%d" % k, [128, 32], F32) for k in range(3)]
        m8 = sbA("m8", [128, 16], F32)
        negsel = [sbA("negsel%d" % k, [128, 32], BF16) for k in range(2)]
        nsT = [sbA("nsT%d" % k, [32, 128], BF16) for k in range(2)]
        ptc = [0]
        smc = [0]

        def post(po, pod, width, i, br, g, o_acc, first):
            pov = po[:, 0:4 * width].rearrange("p (h w) -> p h w", h=4)
            s_ = sm[smc[0] % 4]
            smc[0] += 1
            O.TS("dve", s_[:, 0:4].unsqueeze(2), pov[:, :, 64:65], 1e-30, None, ALU.max, None, [pod], [s_.d()])
            O.RCP(s_[:, 4:8], s_[:, 0:4], [s_.d()], [s_.d()])
            O.TT("dve", s_[:, 8:12], s_[:, 4:8], gs[:, i, br * 8 + 4 * g:br * 8 + 4 * g + 4], ALU.mult, [s_.d(), gs.d()], [s_.d()])
            fb = s_[:, 8:12].unsqueeze(2).to_broadcast([128, 4, 64])
            if first:
                O.TT("dve", o_acc[:, 4 * g:4 * g + 4, :], pov[:, :, 0:64], fb, ALU.mult, [pod, s_.d()], [o_acc.d(g)])
            else:
                t_ = tmpo[smc[0] % 2]
                O.TT("dve", t_[:, :, :], pov[:, :, 0:64], fb, ALU.mult, [pod, s_.d()], [t_.d()])
                O.TT("dve", o_acc[:, 4 * g:4 * g + 4, :], o_acc[:, 4 * g:4 * g + 4, :], t_[:, :, :], ALU.add, [t_.d(), o_acc.d(g)], [o_acc.d(g)])
            return s_, pov

        steps = []
        defer = {}

        def add_defer(idx, fn):
            defer.setdefault(idx, []).append(fn)

        def mk_cmp(i, g, oa, qd):
            st = {}

            def qk():
                pS, pSd = PA.get("S")
                st["pS"] = (pS, pSd)
                O.MM(pS[0:127, :].rearrange("p (h q) -> p h q", h=4), kcmpT[0:65, g, 0:127], qT[0:65, 4 * g:4 * g + 4, i * 128:(i + 1) * 128], True, False,
                     [kcmpT.d()] + qd, [pSd])
                for hh in range(4):
                    O.MM(pS[0:127, hh * 128:(hh + 1) * 128], Gc[:, 4 * g + hh, 120 - 8 * i:247 - 8 * i], ident[:, :], False, hh == 3, [Gc.d(), ident.d()], [pSd])

            def act():
                pS, pSd = st["pS"]
                pt = PT[ptc[0] % len(PT)]
                ptc[0] += 1
                st["pt"] = pt
                O.ACT(pt[0:127, :], pS[0:127, :], AF.Exp, [pSd], [pt.d()])

            def pv(k):
                pt = st["pt"]
                po, pod = PA.get("O")
                for hh in range(4):
                    O.MM(po[:, hh * 97:(hh + 1) * 97], pt[0:127, hh * 128:(hh + 1) * 128], VcX[0:127, g, :], True, True, [pt.d(), VcX.d()], [pod])
                s_, pov = post(po, pod, 97, i, 0, g, oa, True)
                if i >= 8:
                    O.TT("dve", timp[:, :, :], pov[:, :, 65:97], s_[:, 4:8].unsqueeze(2).to_broadcast([128, 4, 32]), ALU.mult, [pod, s_.d()], [timp.d()])
                    P.op("dve", lambda e: e.tensor_reduce(out=sc[0][:, :], in_=timp[:, :, :].rearrange("p h n -> p n h"), axis=AX.X, op=ALU.add),
                         [timp.d()], [sc[0].d()])
                    O.TT("dve", sc[1][:, :], sc[0][:, :], ub[:, i - 8, :], ALU.min, [sc[0].d(), ub.d()], [sc[1].d()])
                    O.TT("dve", sc[1][:, :], sc[1][:, :], lb[:, i - 8, :], ALU.max, [sc[1].d(), lb.d()], [sc[1].d()])
                    P.op("dve", lambda e: e.max(out=m8[:, 0:8], in_=sc[1][:, :]), [sc[1].d()], [m8.d()])
                    P.op("dve", lambda e: e.match_replace(out=sc[2][:, :], in_to_replace=m8[:, 0:8], in_values=sc[1][:, :], imm_value=-1e9),
                         [sc[1].d(), m8.d()], [sc[2].d()])
                    P.op("dve", lambda e: e.max(out=m8[:, 8:16], in_=sc[2][:, :]), [sc[2].d()], [m8.d()])
                    ng = negsel[g]
                    O.TS("dve", ng[:, :], sc[1][:, :], m8[:, 15:16], -30000.0, ALU.is_lt, ALU.mult, [sc[1].d(), m8.d()], [ng.d()])

                    def tr():
                        pb, pd = PA.get("T")
                        pvw = pb.bitcast(BF16)
                        O.TR(pvw[0:32, 0:128], ng[:, :], ident[:, :], [ng.d(), ident.d()], [pd])
                        O.CP("dve", nsT[g][:, :], pvw[0:32, 0:128], [pd], [nsT[g].d()])
                    add_defer(k + 4, tr)

            return (qk, act, pv)

        def mk_att(i, g, br, kT_, V1_, j, idx, njs, grp, oa, qd):
            st = {}

            def qk():
                pS, pSd = PA.get("S")
                st["pS"] = (pS, pSd)
                pS3 = pS.rearrange("p (h q) -> p h q", h=4)
                extra = []
                if j == i:
                    extra.append((ident[:, :], T1d[:, 4 * g:4 * g + 4, 0:128], [ident.d(), T1d.d()]))
                if j == i - 1:
                    extra.append((ident[:, :], T1d[:, 4 * g:4 * g + 4, 128:256], [ident.d(), T1d.d()]))
                if br == 1 and i >= 8:
                    extra.append((Eb[0:32, j * 128:(j + 1) * 128], nsT[g][0:32, :].unsqueeze(1).to_broadcast([32, 4, 128]), [Eb.d(), nsT[g].d()]))
                if br == 2 and j == i - 4:
                    extra.append((ident[:, :], band[:, :].unsqueeze(1).to_broadcast([128, 4, 128]), [ident.d(), band.d()]))
                O.MM(pS3, kT_[0:65, g, j * 128:(j + 1) * 128], qT[0:65, 4 * g:4 * g + 4, i * 128:(i + 1) * 128], True, len(extra) == 0,
                     [kT_.d()] + qd, [pSd])
                for k2, (l_, r_, dd) in enumerate(extra):
                    O.MM(pS3, l_, r_, False, k2 == len(extra) - 1, dd, [pSd])

            def act():
                pS, pSd = st["pS"]
                pt = PT[ptc[0] % len(PT)]
                ptc[0] += 1
                st["pt"] = pt
                O.ACT(pt[:, :], pS, AF.Exp, [pSd], [pt.d()])

            def pv(k):
                pt = st["pt"]
                if idx == 0:
                    grp["po"] = PA.get("O")
                po, pod = grp["po"]
                for hh in range(4):
                    O.MM(po[:, hh * 65:(hh + 1) * 65], pt[:, hh * 128:(hh + 1) * 128], V1_[:, j, g, :], idx == 0 and hh == 0, idx == njs - 1,
                         [pt.d(), V1_.d()], [pod])
                if idx == njs - 1:
                    post(po, pod, 65, i, br, g, oa, False)

            return (qk, act, pv)

        def mk_fin(i, oa):
            def fin():
                ob_ = onsab[i % 2]
                O.CP("act", ob_[:, :], oa[:, :, :].rearrange("p h d -> p (h d)"), [oa.d(0), oa.d(1)], [ob_.d()])
                pb, pd = PA.get("T")
                pvw = pb.bitcast(BF16)
                for c in range(4):
                    O.TR(pvw[:, c * 128:(c + 1) * 128], ob_[:, c * 128:(c + 1) * 128], ident[:, :], [ob_.d(), ident.d()], [pd])
                O.CP("dve", onsaT[:, :, i * 128:(i + 1) * 128], pvw[:, 0:512].rearrange("p (c t) -> p c t", c=4), [pd], [onsaT.d(i)])
            return fin

        for i in range(16):
            oa = onsa[i % 2]
            qd = [qT.d(i // 4)]
            for g in range(2):
                steps.append(mk_cmp(i, g, oa, qd))
            for br, kT_, V1_ in ((2, kwT, Vw1), (1, ksT, Vs1)):
                for g in range(2):
                    js = list(range(max(0, i - 4), i + 1)) if br == 2 else list(range(0, i + 1))
                    grp = {}
                    for idx, j in enumerate(js):
                        steps.append(mk_att(i, g, br, kT_, V1_, j, idx, len(js), grp, oa, qd))
            add_defer(len(steps) - 1 + 3, mk_fin(i, oa))
        LA = 3
        PA.mode = "att2"
        nst = len(steps)
        for k in range(min(LA, nst)):
            steps[k][0]()
        for k in range(nst):
            if k + LA < nst:
                steps[k + LA][0]()
            steps[k][1]()
            steps[k][2](k)
            for fn in defer.pop(k, []):
                fn()
        for k in sorted(defer):
            for fn in defer[k]:
                fn()
        PA.mode = "att"
        P.barrier()


def mem_phase(P, O, nc, PA, C, W, uT, omemT, mem_rows, after_weights=None):
    ident = C["ident"]
    w_inv = W["w_in"].rearrange("(c p) f -> p c f", p=128)
    wkv_v = W["w_mem_kv"].rearrange("(c p) f -> p c f", p=128)
    ug = lambda tg: [uT.d(4 * tg + k) for k in range(4)]
    UID[0] += 1
    with contextlib.ExitStack() as esB:
        sbB = lambda name, shape, dt: Buf(esB.enter_context(nc.sbuf_tensor("B_%d_" % UID[0] + name, shape, dt)))
        wkv = sbB("wkv", [128, 8, 1024], BF16)
        winM = sbB("winM", [128, 8, 512], BF16)
        for c in range(8):
            O.DMA("pool", wkv[:, c, :], wkv_v[:, c, :], [], [wkv.d(c)])
            O.DMA("pool", winM[:, c, :], w_inv[:, c, 3360:3872], [], [winM.d(c)])
        if after_weights is not None:
            after_weights()
        gmem = sbB("gmem", [128, D], F32)
        O.DMA("sp", gmem[:, :], W["mem_norm"].partition_broadcast(128), [], [gmem.d()])
        tmp = {"xt": [sbB("xt%d" % i, [128, D], F32) for i in range(2)], "xs": [sbB("xs%d" % i, [128, D], BF16) for i in range(1)],
               "st": [sbB("st%d" % i, [128, 8], F32) for i in range(4)], "junk": sbB("junk", [128, D], BF16)}
        memnT = sbB("memnT", [128, 8, 256], BF16)
        for t in range(2):
            norm_rows_T(P, O, nc, PA, C, tmp, mem_rows[t * 128:(t + 1) * 128, :], gmem, memnT, t * 128, t)
        md = [memnT.d(0), memnT.d(1)]
        kmT = sbB("kmT", [128, 4, 256], BF16)
        Vm1 = sbB("Vm1", [128, 2, 4, 129], BF16)
        mqT = sbB("mqT", [128, 4, 2048], BF16)
        for h in range(4):
            pb, pd = PA.get("M")
            for c in range(8):
                O.MM(pb[:, 0:256], wkv[:, c, h * 128:(h + 1) * 128], memnT[:, c, :], c == 0, c == 7, [wkv.d(c)] + md, [pd])
            O.CP("dve", kmT[:, h, :], pb[:, 0:256], [pd], [kmT.d()])
        for mt in range(2):
            pb, pd = PA.get("M")
            for c in range(8):
                O.MM(pb[:, :], memnT[:, c, mt * 128:(mt + 1) * 128], wkv[:, c, 512:1024], c == 0, c == 7, [wkv.d(c)] + md, [pd])
            O.CP("dve", Vm1[:, mt, :, 0:128], pb.rearrange("p (h d) -> p h d", h=4), [pd], [Vm1.d()])
        O.MS("dve", Vm1[:, :, :, 128:129], 1.0, [Vm1.d()])
        for h in range(4):
            for tg in range(4):
                pb, pd = PA.get("M")
                for c in range(8):
                    O.MM(pb[:, :], winM[:, c, h * 128:(h + 1) * 128], uT[:, c, tg * 512:(tg + 1) * 512], c == 0, c == 7, [winM.d(c)] + ug(tg), [pd])
                P.op("act", lambda e, h=h, tg=tg, pb=pb: e.mul(out=mqT[:, h, tg * 512:(tg + 1) * 512], in_=pb[:, :], mul=128 ** -0.5), [pd], [mqT.d(tg)])
        omem = [sbB("omem%d" % k, [128, 4, 128], BF16) for k in range(4)]
        PTm = [sbB("PTm%d" % k, [128, 512], BF16) for k in range(4)]
        sm = [sbB("sm%d" % k, [128, 4], F32) for k in range(4)]
        k_ = 0
        for tg in range(4):
            for h in range(4):
                pts = []
                for mt in range(2):
                    pS, pSd = PA.get("S")
                    O.MM(pS, kmT[:, h, mt * 128:(mt + 1) * 128], mqT[:, h, tg * 512:(tg + 1) * 512], True, True, [kmT.d(), mqT.d(tg)], [pSd])
                    pt = PTm[(h * 2 + mt) % 4]
                    O.ACT(pt[:, :], pS, AF.Exp, [pSd], [pt.d()])
                    pts.append(pt)
                for qt in range(4):
                    po, pod = PA.get("O")
                    for mt in range(2):
                        O.MM(po[:, 0:129], pts[mt][:, qt * 128:(qt + 1) * 128], Vm1[:, mt, h, :], mt == 0, mt == 1, [pts[mt].d(), Vm1.d()], [pod])
                    s_ = sm[k_ % 4]
                    k_ += 1
                    O.RCP(s_[:, 0:1], po[:, 128:129], [pod], [s_.d()])
                    O.TS("dve", omem[qt][:, h, :], po[:, 0:128], s_[:, 0:1], None, ALU.mult, None, [pod, s_.d()], [omem[qt].d()])
            for qt in range(4):
                t = tg * 4 + qt
                pb, pd = PA.get("T")
                pv = pb.bitcast(BF16)
                om = omem[qt][:, :, :].rearrange("p h d -> p (h d)")
                for c in range(4):
                    O.TR(pv[:, c * 128:(c + 1) * 128], om[:, c * 128:(c + 1) * 128], ident[:, :], [omem[qt].d(), ident.d()], [pd])
                O.CP("dve", omemT[:, :, t * 128:(t + 1) * 128], pv[:, 0:512].rearrange("p (c t) -> p c t", c=4), [pd], [omemT.d(t)])
        P.barrier()


def merge_phase(P, O, nc, PA, C, W, uT, obr, x1_rows, x2_rows, ps):
    wg_v = W["w_branch_gate"].rearrange("(c p) f -> p c f", p=128)
    wb_v = [W[k].rearrange("(c p) f -> p c f", p=128) for k in ("w_branch_nsa", "w_branch_dn", "w_branch_mem")]
    wo_v = W["w_out"].rearrange("(c p) f -> p c f", p=128)
    ug = lambda tg: [uT.d(4 * tg + k) for k in range(4)]
    UID[0] += 1
    with contextlib.ExitStack() as esD:
        sbD = lambda name, shape, dt: Buf(esD.enter_context(nc.sbuf_tensor("D_%d_" % UID[0] + name, shape, dt)))
        wout = sbD("wout", [128, 8, D], BF16)
        for c in range(8):
            O.DMA("pool", wout[:, c, :], wo_v[:, c, :], [], [wout.d(c)])
        gpm = sbD("gpm", [128, D], F32)
        O.DMA("sp", gpm[:, :], W["mix_post_norm"].partition_broadcast(128), [], [gpm.d()])
        mT = sbD("mT", [128, 8, SEQ], BF16)
        wgd = [sbD("wgd%d" % k, [128, 8, 3, 128], BF16) for k in range(2)]
        wbd = [sbD("wbd%d" % k, [128, 3, 4, 128], BF16) for k in range(2)]
        sg = [sbD("sg%d" % k, [128, 512], F32) for k in range(2)]
        macc = sbD("macc", [128, 512], F32)
        tmpm = sbD("tmpm", [128, 512], F32)
        k_ = 0
        for dc in range(8):
            wg_ = wgd[dc % 2]
            wb_ = wbd[dc % 2]
            for br in range(3):
                O.DMA("pool", wg_[:, :, br, :], wg_v[:, :, br * 1024 + dc * 128:br * 1024 + dc * 128 + 128], [], [wg_.d(br)])
                O.DMA("pool", wb_[:, br, :, :], wb_v[br][:, :, dc * 128:(dc + 1) * 128], [], [wb_.d(br)])
            for tg in range(4):
                for br in range(3):
                    pG, pGd = PA.get("S")
                    for c in range(8):
                        O.MM(pG, wg_[:, c, br, :], uT[:, c, tg * 512:(tg + 1) * 512], c == 0, c == 7, [wg_.d(br)] + ug(tg), [pGd])
                    pY, pYd = PA.get("O")
                    od = [obr[br].d(4 * tg + k) for k in range(4)]
                    for kc in range(4):
                        O.MM(pY, wb_[:, br, kc, :], obr[br][:, kc, tg * 512:(tg + 1) * 512], kc == 0, kc == 3, [wb_.d(br)] + od, [pYd])
                    s_ = sg[k_ % 2]
                    k_ += 1
                    O.ACT(s_[:, :], pG, AF.Sigmoid, [pGd], [s_.d()])
                    if br == 0:
                        O.TT("dve", macc[:, :], pY, s_[:, :], ALU.mult, [pYd, s_.d()], [macc.d()])
                    else:
                        O.TT("dve", tmpm[:, :], pY, s_[:, :], ALU.mult, [pYd, s_.d()], [tmpm.d()])
                        if br == 1:
                            O.TT("dve", macc[:, :], macc[:, :], tmpm[:, :], ALU.add, [macc.d(), tmpm.d()], [macc.d()])
                        else:
                            O.TT("dve", mT[:, dc, tg * 512:(tg + 1) * 512], macc[:, :], tmpm[:, :], ALU.add, [macc.d(), tmpm.d()], [mT.d(tg)])
        xr = [sbD("xr%d" % k, [128, D], F32) for k in range(2)]
        ob = [sbD("ob%d" % k, [128, D], F32) for k in range(2)]
        st = [sbD("st%d" % k, [128, 8], F32) for k in range(2)]
        junk = sbD("junk", [128, D], BF16)
        for t in range(16):
            pd = ps[2] if t % 2 == 0 else ps[0]
            for hf in range(2):
                for dc in range(8):
                    O.MM(pd[:, hf * 512:(hf + 1) * 512], mT[:, dc, t * 128:(t + 1) * 128], wout[:, dc, hf * 512:(hf + 1) * 512], dc == 0, dc == 7,
                         [mT.d(t // 4), wout.d(dc)], [pd.d(hf)])
            s = st[t % 2]
            x_ = xr[t % 2]
            o_ = ob[t % 2]
            O.DMA("sp", x_[:, :], x1_rows[t * 128:(t + 1) * 128, :], [], [x_.d()])
            O.CP("act", o_[:, :], pd[:, :], [pd.d(0), pd.d(1)], [o_.d()])
            O.ACT(junk[:, :], o_[:, :], AF.Square, [o_.d()], [junk.d(), s.d()], accum=s[:, 0:1])
            rstd_ops(O, s, slice(0, 1), slice(1, 2), slice(2, 3), slice(3, 4), D)
            O.STT("dve", o_[:, :], o_[:, :], s[:, 3:4], gpm[:, :], ALU.mult, ALU.mult, [o_.d(), s.d(), gpm.d()], [o_.d()])
            O.TT("dve", o_[:, :], o_[:, :], x_[:, :], ALU.add, [o_.d(), x_.d()], [o_.d()])
            O.DMA("pool", x2_rows[t * 128:(t + 1) * 128, :], o_[:, :], [o_.d()], [], is_output=True)
        P.barrier()


def dn_phase(P, O, nc, PA, C, W, uT, odnT, winD_pre=None):
    ident = C["ident"]
    cf = C["cf"]
    cfd = [cf.d()]
    UTm, ONES, SLm, UPm = cf[:, 1, :], cf[:, 2, :], cf[:, 3, :], cf[:, 4, :]
    w_inv = W["w_in"].rearrange("(c p) f -> p c f", p=128)
    uall = [uT.d(k) for k in range(16)]
    ug = lambda tg: [uT.d(4 * tg + k) for k in range(4)]
    UID[0] += 1
    with contextlib.ExitStack() as esC:
        sbC = lambda name, shape, dt: Buf(esC.enter_context(nc.sbuf_tensor("C_%d_" % UID[0] + name, shape, dt)))
        if winD_pre is not None:
            winD = winD_pre
        else:
            winD = sbC("winD", [128, 8, 2056], BF16)
            for c in range(8):
                O.DMA("pool", winD[:, c, :], w_inv[:, c, 1304:3360], [], [winD.d(c)])
        wD = [winD.d(c) for c in range(8)]
        cw = sbC("cw", [128, 12, 4], F32)
        O.DMA("sp", cw[:, :, :], W["dn_conv_wT"], [], [cw.d()])
        sv = sbC("sv", [128, 16], F32)
        gdn = sbC("gdn", [128, 128], F32)
        O.DMA("sp", sv[:, 0:4], W["dn_a_log"].partition_broadcast(128), [], [sv.d()])
        O.DMA("sp", sv[:, 4:8], W["dn_dt_bias"].partition_broadcast(128), [], [sv.d()])
        O.DMA("sp", gdn[:, :], W["dn_out_norm"].partition_broadcast(128), [], [gdn.d()])
        onesb = sbC("onesb", [128, 128], BF16)
        O.MS("dve", onesb[:, :], 1.0, [onesb.d()])
        O.ACT(sv[:, 8:12], sv[:, 0:4], AF.Exp, [sv.d()], [sv.d()])
        O.TS("dve", sv[:, 12:16], sv[:, 8:12], -1.0, None, ALU.mult, None, [sv.d()], [sv.d()])
        ab = sbC("ab", [128, 16, 8], F32)
        for t in range(16):
            pb, pd = PA.get("M")
            for c in range(8):
                O.MM(pb[:, 0:8], uT[:, c, t * 128:(t + 1) * 128], winD[:, c, 1536:1544], c == 0, c == 7, [winD.d(c), uT.d(t)], [pd])
            O.CP("dve", ab[:, t, :], pb[:, 0:8], [pd], [ab.d()])
        gall = sbC("gall", [128, 16, 4], F32)
        beta = sbC("beta", [128, 16, 4], F32)
        nbeta = sbC("nbeta", [128, 16, 4], F32)
        O.TT("dve", gall[:, :, :], ab[:, :, 0:4], sv[:, 4:8].unsqueeze(1).to_broadcast([128, 16, 4]), ALU.add, [ab.d(), sv.d()], [gall.d()])
        O.ACT(gall[:, :, :], gall[:, :, :], AF.Exp, [gall.d()], [gall.d()])
        O.ACT(gall[:, :, :], gall[:, :, :], AF.Ln, [gall.d()], [gall.d()], bias=1.0)
        O.TT("dve", gall[:, :, :], gall[:, :, :], sv[:, 12:16].unsqueeze(1).to_broadcast([128, 16, 4]), ALU.mult, [gall.d(), sv.d()], [gall.d()])
        O.ACT(beta[:, :, :], ab[:, :, 4:8], AF.Sigmoid, [ab.d()], [beta.d()])
        O.TS("dve", nbeta[:, :, :], beta[:, :, :], -1.0, None, ALU.mult, None, [beta.d()], [nbeta.d()])
        qkv = [sbC("qcT", [128, SEQ], BF16), sbC("kT", [128, SEQ], BF16), sbC("vT", [128, SEQ], BF16)]
        qcT, kT, vT = qkv
        zs = sbC("zs", [128, 16, 128], BF16)
        S = sbC("S", [128, 128], F32)
        Sb = sbC("Sb", [128, 128], BF16)
        G_ = 4
        for h in range(4):
            with contextlib.ExitStack() as eh1:
                sb1 = lambda name, shape, dt: Buf(eh1.enter_context(nc.sbuf_tensor("C1_%d_%d_" % (UID[0], h) + name, shape, dt)))
                pre_l = [sb1("pre%d" % k, [128, 3 + SEQ], BF16) for k in range(2)]
                acc_l = [sb1("acc%d" % k, [128, SEQ], F32) for k in range(2)]
                sq_l = [sb1("sq%d" % k, [128, SEQ], BF16) for k in range(2)]
                rs = [sb1("rs%d" % k, [128, 512], F32) for k in range(2)]
                dg = sb1("dg", [128, 12, 128], BF16)
                for pre in pre_l:
                    O.MS("dve", pre[:, 0:3], 0.0, [pre.d(-1)])
                for which in range(3):
                    for i in range(4):
                        O.TS("dve", dg[:, which * 4 + i, :], cf[:, 0, :], cw[:, which * 4 + h, i:i + 1], None, ALU.mult, None, cfd + [cw.d()], [dg.d(which)])
                for which in range(3):
                    cc = which * 4 + h
                    col0 = cc * 128
                    pre, acc, sq = pre_l[which % 2], acc_l[which % 2], sq_l[which % 2]
                    for tg in range(4):
                        pb, pd = PA.get("M")
                        for c in range(8):
                            O.MM(pb, winD[:, c, col0:col0 + 128], uT[:, c, tg * 512:(tg + 1) * 512], c == 0, c == 7, [winD.d(c)] + ug(tg), [pd])
                        O.CP("act", pre[:, 3 + tg * 512:3 + (tg + 1) * 512], pb, [pd], [pre.d(tg)])
                    dst = qkv[which]
                    for tg in range(4):
                        sl = slice(tg * 512, (tg + 1) * 512)
                        pc, pcd = PA.get("M")
                        for i in range(4):
                            O.MM(pc, dg[:, which * 4 + i, :], pre[:, tg * 512 + i:tg * 512 + i + 512], i == 0, i == 3,
                                 [dg.d(which), pre.d(tg - 1), pre.d(tg)], [pcd])
                        if which == 2:
                            O.ACT(vT[:, sl], pc, AF.Silu, [pcd], [vT.d()])
                        else:
                            O.ACT(acc[:, sl], pc, AF.Silu, [pcd], [acc.d(tg)])
                            O.TT("dve", sq[:, sl], acc[:, sl], acc[:, sl], ALU.mult, [acc.d(tg)], [sq.d(tg)])
                    if which != 2:
                        for tg in range(4):
                            pb, pd = PA.get("M")
                            O.MM(pb, onesb[:, :], sq[:, tg * 512:(tg + 1) * 512], True, True, [onesb.d(), sq.d(tg)], [pd])
                            r_ = rs[tg % 2]
                            O.TS("dve", r_[:, :], pb, EPS, None, ALU.add, None, [pd], [r_.d()])
                            O.ACT(r_[:, :], r_[:, :], AF.Ln, [r_.d()], [r_.d()])
                            O.ACT(r_[:, :], r_[:, :], AF.Exp, [r_.d()], [r_.d()], scale=-0.5)
                            sl = slice(tg * 512, (tg + 1) * 512)
                            if which == 0:
                                O.STT("dve", dst[:, sl], acc[:, sl], 128 ** -0.5, r_[:, :], ALU.mult, ALU.mult, [acc.d(tg), r_.d()], [dst.d()])
                            else:
                                O.TT("dve", dst[:, sl], acc[:, sl], r_[:, :], ALU.mult, [acc.d(tg), r_.d()], [dst.d()])
                for t in range(16):
                    pb, pd = PA.get("M")
                    for c in range(8):
                        O.MM(pb[:, 0:128], uT[:, c, t * 128:(t + 1) * 128], winD[:, c, 1544 + h * 128:1544 + (h + 1) * 128], c == 0, c == 7,
                             [winD.d(c), uT.d(t)], [pd])
                    O.ACT(zs[:, t, :], pb[:, 0:128], AF.Silu, [pd], [zs.d()])
                P.barrier()
            with contextlib.ExitStack() as eh2:
                sb2 = lambda name, shape, dt: Buf(eh2.enter_context(nc.sbuf_tensor("C2_%d_%d_" % (UID[0], h) + name, shape, dt)))
                f32t = lambda nm, n_: [sb2("%s%d" % (nm, k), [128, 128], F32) for k in range(n_)]
                bft = lambda nm, n_: [sb2("%s%d" % (nm, k), [128, 128], BF16) for k in range(n_)]
                Gm, Dm, DTm, Er, t3, t4 = (f32t(n, G_) for n in ("Gm", "Dm", "DTm", "Er", "t3", "t4"))
                Pb = [[sb2("Pb%d_%d" % (k, j), [128, 256], F32) for j in range(2)] for k in range(G_)]
                yb = [[sb2("yb%d_%d" % (k, j), [128, 256], F32) for j in range(2)] for k in range(G_)]
                qdT, qkT, kdec, WT, Ub = (bft(n, 2 * G_) for n in ("qdT", "qkT", "kdec", "WT", "Ub"))
                cs = [sb2("cs%d" % k, [128, 12], F32) for k in range(2 * G_)]
                vnew, onb = (bft(n, 2) for n in ("vnew", "onb"))
                on_ = f32t("on", 2)
                junk = sb2("junk", [128, 128], BF16)
                O.MS("dve", S[:, :], 0.0, [S.d()])
                O.MS("dve", Sb[:, :], 0.0, [Sb.d()])
                ca = sb2("ca", [128, 6, 16], F32)
                pbm, pbmd = PA.get("M")
                O.MM(pbm[:, 0:16], UTm, gall[:, :, h], True, True, cfd + [gall.d()], [pbmd])
                O.MM(pbm[:, 16:32], ONES, gall[:, :, h], True, True, cfd + [gall.d()], [pbmd])
                O.CP("dve", ca[:, 0:2, :], pbm[:, 0:32].rearrange("p (a n) -> p a n", a=2), [pbmd], [ca.d()])
                O.ACT(ca[:, 2, :], ca[:, 0, :], AF.Exp, [ca.d()], [ca.d()])
                O.TT("dve", ca[:, 3, :], ca[:, 1, :], ca[:, 0, :], ALU.subtract, [ca.d()], [ca.d()])
                O.ACT(ca[:, 3, :], ca[:, 3, :], AF.Exp, [ca.d()], [ca.d()])
                O.ACT(ca[:, 4, :], ca[:, 1, :], AF.Exp, [ca.d()], [ca.d()])
                O.TT("dve", ca[:, 5, :], beta[:, :, h], ca[:, 2, :], ALU.mult, [beta.d(), ca.d()], [ca.d()])

                def pre_gen(n):
                    k = n % G_
                    pk = n % (2 * G_)
                    tl = slice(n * 128, (n + 1) * 128)
                    c_ = cs[pk]
                    g_col = gall[:, n, h:h + 1]
                    b_col = beta[:, n, h:h + 1]
                    nb_col = nbeta[:, n, h:h + 1]
                    bk = PA.ps[k // 2]
                    bank = bk[:, (k % 2) * 512:(k % 2 + 1) * 512]
                    bd = bk.d(k % 2)
                    pK = bank[:, 0:256]
                    pR = bank[:, 256:384]
                    pb = bank[:, 256:258]
                    pv = bank.bitcast(BF16)[:, 768:1024]
                    O.TS("dve", Gm[k][:, :], ONES, g_col, None, ALU.mult, None, cfd + [gall.d()], [Gm[k].d()])
                    O.MM(pK[:, 0:128], kT[:, tl], kT[:, tl], True, True, [kT.d()], [bd])
                    O.MM(pK[:, 128:256], kT[:, tl], qcT[:, tl], True, True, [kT.d(), qcT.d()], [bd])
                    O.TR(pv[:, 0:128], kT[:, tl], ident[:, :], [kT.d(), ident.d()], [bd])
                    O.TR(pv[:, 128:256], vT[:, tl], ident[:, :], [vT.d(), ident.d()], [bd])
                    O.MM(pR, Gm[k][:, :], UTm, True, True, cfd + [Gm[k].d()], [bd])
                    yield
                    O.TS("dve", Dm[k][:, :], pR, ca[:, 0, n:n + 1], 0.0, ALU.subtract, ALU.max, [bd, ca.d()], [Dm[k].d(), bd])
                    O.TS("dve", DTm[k][:, :], pR, ca[:, 0, n:n + 1], 0.0, ALU.subtract, ALU.min, [bd, ca.d()], [DTm[k].d(), bd])
                    O.ACT(Er[k][:, :], pR, AF.Exp, [bd], [Er[k].d(), bd])
                    yield
                    O.ACT(Dm[k][:, :], Dm[k][:, :], AF.Exp, [Dm[k].d()], [Dm[k].d()], scale=-1.0)
                    O.ACT(DTm[k][:, :], DTm[k][:, :], AF.Exp, [DTm[k].d()], [DTm[k].d()])
                    y0 = yb[k][0]
                    O.TS("dve", y0[:, 128:256], pv[:, 0:128], ca[:, 5, n:n + 1], None, ALU.mult, None, [bd, ca.d()], [y0.d(), bd])
                    O.TS("dve", kdec[pk][:, :], pv[:, 0:128], ca[:, 3, n:n + 1], None, ALU.mult, None, [bd, ca.d()], [kdec[pk].d(), bd])
                    O.TS("dve", y0[:, 0:128], pv[:, 128:256], b_col, None, ALU.mult, None, [bd, beta.d()], [y0.d(), bd])
                    O.TT("dve", qdT[pk][:, :], qcT[:, tl], Er[k][:, :], ALU.mult, [qcT.d(), Er[k].d()], [qdT[pk].d()])
                    yield
                    P0 = Pb[k][0]
                    O.STT("dve", t3[k][:, :], pK[:, 0:128], nb_col, Dm[k][:, :], ALU.mult, ALU.mult, [bd, nbeta.d(), Dm[k].d()], [t3[k].d(), bd])
                    O.TT("dve", t4[k][:, :], pK[:, 128:256], DTm[k][:, :], ALU.mult, [bd, DTm[k].d()], [t4[k].d(), bd])
                    yield
                    O.TT("dve", P0[:, 0:128], t3[k][:, :], SLm, ALU.mult, [t3[k].d()] + cfd, [P0.d()])
                    O.TT("dve", qkT[pk][:, :], t4[k][:, :], UPm, ALU.mult, [t4[k].d()] + cfd, [qkT[pk].d()])
                    yield
                    O.TR(bank[:, 0:128], P0[:, 0:128], cf[:, 0, :], [P0.d()] + cfd, [bd])
                    yield
                    O.CP("act", P0[:, 128:256], bank[:, 0:128], [bd], [P0.d(), bd])
                    yield
                    Pc = P0
                    y = y0
                    pa = bank[:, 0:256]
                    pq = bank[:, 256:512]
                    for l in range(7):
                        O.MM(pa, Pc[:, 128:256], y[:, :], True, True, [Pc.d(), y.d()], [bd])
                        if l < 6:
                            O.MM(pq[:, 0:128], Pc[:, 128:256], Pc[:, 0:128], True, True, [Pc.d()], [bd])
                            O.MM(pq[:, 128:256], Pc[:, 0:128], Pc[:, 128:256], True, True, [Pc.d()], [bd])
                        yield
                        yn = yb[k][(l + 1) % 2]
                        O.TT("dve", yn[:, :], y[:, :], pa, ALU.add, [y.d(), bd], [yn.d(), bd])
                        if l < 6:
                            Pn = Pb[k][(l + 1) % 2]
                            O.CP("act", Pn[:, :], pq, [bd], [Pn.d(), bd])
                            Pc = Pn
                        y = yn
                        yield
                    O.TR(bank[:, 0:128], y[:, 128:256], cf[:, 0, :], [y.d()] + cfd, [bd])
                    O.CP("act", Ub[pk][:, :], y[:, 0:128], [y.d()], [Ub[pk].d()])
                    yield
                    O.CP("act", WT[pk][:, :], bank[:, 0:128], [bd], [WT[pk].d(), bd])

                def scan_gen(ns):
                    p1, p1d = PA.ps[2][:, 0:512], PA.ps[2].d(0)
                    p2, p2d = PA.ps[2][:, 512:1024], PA.ps[2].d(1)
                    p3, p3d = PA.ps[3][:, 0:512], PA.ps[3].d(0)
                    pt4, pt4d = PA.ps[3][:, 512:1024], PA.ps[3].d(1)
                    for n in ns:
                        pk = n % (2 * G_)
                        k2 = n % 2
                        tl = slice(n * 128, (n + 1) * 128)
                        c_ = cs[pk]
                        O.MM(p1[:, 0:128], WT[pk][:, :], Sb[:, :], True, True, [WT[pk].d(), Sb.d()], [p1d])
                        yield
                        O.TT("dve", vnew[k2][:, :], Ub[pk][:, :], p1[:, 0:128], ALU.subtract, [Ub[pk].d(), p1d], [vnew[k2].d()])
                        yield
                        O.MM(p2[:, 0:128], qdT[pk][:, :], Sb[:, :], True, False, [qdT[pk].d(), Sb.d()], [p2d])
                        O.MM(p2[:, 0:128], qkT[pk][:, :], vnew[k2][:, :], False, True, [qkT[pk].d(), vnew[k2].d()], [p2d])
                        O.MM(p3[:, 0:128], kdec[pk][:, :], vnew[k2][:, :], True, True, [kdec[pk].d(), vnew[k2].d()], [p3d])
                        yield
                        O.STT("dve", S[:, :], S[:, :], ca[:, 4, n:n + 1], p3[:, 0:128], ALU.mult, ALU.add, [S.d(), ca.d(), p3d], [S.d()])
                        O.ACT(junk[:, :], p2[:, 0:128], AF.Square, [p2d], [junk.d(), c_.d()], accum=c_[:, 6:7])
                        yield
                        O.CP("act", Sb[:, :], S[:, :], [S.d()], [Sb.d()])
                        rstd_ops(O, c_, slice(6, 7), slice(7, 8), slice(8, 9), slice(9, 10), 128)
                        yield
                        O.STT("dve", on_[k2][:, :], p2[:, 0:128], c_[:, 9:10], gdn[:, :], ALU.mult, ALU.mult, [p2d, c_.d(), gdn.d()], [on_[k2].d()])
                        O.TT("dve", onb[k2][:, :], on_[k2][:, :], zs[:, n, :], ALU.mult, [on_[k2].d(), zs.d()], [onb[k2].d()])
                        yield
                        pv4 = pt4.bitcast(BF16)
                        O.TR(pv4[:, 0:128], onb[k2][:, :], ident[:, :], [onb[k2].d(), ident.d()], [pt4d])
                        yield
                        O.CP("act", odnT[:, h, tl], pv4[:, 0:128], [pt4d], [odnT.d(n)])

                def run_rr(gens):
                    gens = list(gens)
                    while gens:
                        for g_ in list(gens):
                            try:
                                next(g_)
                            except StopIteration:
                                gens.remove(g_)

                prev = None
                for g0 in range(0, 16, G_):
                    ns = list(range(g0, g0 + G_))
                    gens = [pre_gen(n) for n in ns]
                    if prev is not None:
                        gens.append(prev)
                    run_rr(gens)
                    prev = scan_gen(ns)
                run_rr([prev])
                P.barrier()
        P.barrier()


def _bucket(dist):
    dist = np.asarray(dist, dtype=np.int64)
    d = np.maximum(dist, 1).astype(np.float32)
    lb = 16 + (np.log(d / np.float32(16.0)).astype(np.float32) / np.float32(np.log(8.0)) * np.float32(16.0)).astype(np.int32)
    return np.where(dist < 16, dist, np.minimum(lb, 31)).astype(np.float32)


def host_consts():
    bf = ml_dtypes.bfloat16
    c = {}
    c["c_ident"] = np.eye(128, dtype=np.float32).astype(bf)
    ii = np.arange(128)
    cf = np.zeros((128, 5, 128), np.float32)
    cf[:, 0, :] = np.eye(128)
    cf[:, 1, :] = (ii[:, None] <= ii[None, :])
    cf[:, 2, :] = 1.0
    cf[:, 3, :] = (ii[:, None] > ii[None, :])
    cf[:, 4, :] = (ii[None, :] >= ii[:, None])
    c["c_f32"] = cf
    k = ii[:, None]
    q = ii[None, :]
    idx1 = np.zeros((128, 2, 128), np.float32)
    mask1 = np.zeros((128, 2, 128), np.float32)
    r0 = q - k
    idx1[:, 0, :] = _bucket(np.maximum(r0, 0))
    mask1[:, 0, :] = np.where(r0 >= 0, 0.0, -30000.0)
    idx1[:, 0, :] = np.where(r0 >= 0, idx1[:, 0, :], 31.0)
    idx1[:, 1, :] = _bucket(q - k + 128)
    c["c_idx1"] = idx1.reshape(128, 256)
    c["c_mask1"] = mask1.reshape(128, 256)
    cp = np.arange(247)[None, :] - 120
    dist = ii[:, None] - 16 * cp - 31
    c["c_idxc"] = np.where(dist >= 0, _bucket(np.maximum(dist, 0)), 31.0).astype(np.float32)
    c["c_maskc"] = np.where(dist >= 0, 0.0, -30000.0).astype(np.float32)
    E = np.zeros((32, 2048), np.float32)
    for n in range(32):
        E[n, n * 64:(n + 1) * 64] = 1.0
    c["c_E"] = E.astype(bf)
    c["c_band"] = np.where(k > q, 0.0, -30000.0).astype(np.float32).astype(bf)
    ub = np.zeros((128, 8, 32), np.float32)
    lb = np.zeros((128, 8, 32), np.float32)
    blk = np.arange(32)[None, :]
    for t in range(8):
        cur = ((8 + t) * 128 + ii) // 64
        cur = cur[:, None]
        causal = blk <= cur
        ub[:, t, :] = np.where(causal, 1e4, -1e4)
        l = np.full((128, 32), -1e4, np.float32)
        l = np.where(blk == cur - 1, 1e4, l)
        l = np.where(blk == cur, 2e4, l)
        l = np.where(blk == 0, 3e4, l)
        lb[:, t, :] = l
    c["c_ub"] = ub
    c["c_lb"] = lb
    c0 = np.arange(127) * 16
    s0 = np.arange(32) * 64
    ov = np.maximum(np.minimum(c0[:, None] + 32, s0[None, :] + 64) - np.maximum(c0[:, None], s0[None, :]), 0)
    ts = np.zeros((128, 32), np.float32)
    ts[:127] = ov.astype(np.float32) / 32.0
    c["c_tosel"] = ts.astype(bf)
    return c


WNAMES = {
    "w_in": [D, 3872], "cmp_k_w1": [2048, 256], "cmp_v_w1": [2048, 256], "cmp_k_w2": [256, 64], "cmp_v_w2": [256, 64],
    "cmp_pos_kT": [64, 32], "cmp_pos_vT": [64, 32], "dn_conv_wT": [128, 12, 4], "dn_a_log": [1, 4], "dn_dt_bias": [1, 4],
    "dn_out_norm": [1, 128], "mem_norm": [1, D], "w_mem_kv": [D, D], "w_branch_nsa": [512, D], "w_branch_dn": [512, D],
    "w_branch_mem": [512, D], "w_branch_gate": [D, 3 * D], "w_out": [D, D], "mix_post_norm": [1, D], "mix_pre_norm": [1, D],
}


SKIP = set()


def build_nc(nseq, dbg=False):
    nc = bass.Bass("TRN2", target_bir_lowering=False)
    ntok = nseq * SEQ
    dr = lambda name, shape, kind="ExternalInput", dt=F32: nc.dram_tensor(name, shape, dt, kind=kind).ap()
    x = dr("x", [ntok, D])
    mem = dr("mem", [nseq * 256, D])
    out = dr("out", [ntok, D], kind="ExternalOutput")
    X1 = out
    X2 = out
    if dbg:
        dbo = [dr("dbg_obr%d" % k, [128, 4, SEQ], kind="ExternalOutput", dt=BF16) for k in range(3)]
    F = {}
    for p in ("ffn1", "ffn2"):
        F[p] = dict(pre=dr(p + "_pre_norm", [1, D]), wg=dr(p + "_w_gate", [D, DFF]), wu=dr(p + "_w_up", [D, DFF]),
                    wd=dr(p + "_w_down", [DFF, D]), post=dr(p + "_post_norm", [1, D]))
    W = {k: dr(k, v) for k, v in WNAMES.items()}
    hc = host_consts()
    CD = {k: dr(k, list(v.shape), dt=(F32 if v.dtype == np.float32 else BF16)) for k, v in hc.items()}
    CD["rel_bias"] = dr("rel_bias", [1, 256])
    with contextlib.ExitStack() as es:
        P = Prog(nc, es)
        O = Ops(P)
        sb = lambda e_, name, shape, dt: Buf(e_.enter_context(nc.sbuf_tensor(name, shape, dt)))
        ident = sb(es, "ident", [128, 128], BF16)
        ps = [Buf(es.enter_context(nc.psum_tensor("ps%d" % i, [128, 1024], F32)), excl=True) for i in range(4)]
        PA = PsumAlloc(ps)
        O.DMA("sp", ident[:, :], CD["c_ident"], [], [ident.d()])
        G["nhalf"] = sb(es, "nhalf", [128, 8], F32)
        O.MS("pool", G["nhalf"][:, :], -0.5, [G["nhalf"].d()])
        with contextlib.ExitStack() as e1:
            f = F["ffn1"]
            ffn_phase(P, nc, e1, x, X1, f["wg"], f["wu"], f["wd"], f["pre"], f["post"], ident, ps, ntok, "f1_")
            P.barrier()
        with contextlib.ExitStack() as e2:
            C = setup_consts(P, O, nc, e2, CD, ident)
            uT = sb(e2, "uT", [128, 8, SEQ], BF16)
            obr = [sb(e2, "obrT%d" % k, [128, 4, SEQ], BF16) for k in range(3)]
            gmix = sb(e2, "gmix", [128, D], F32)
            O.DMA("sp", gmix[:, :], W["mix_pre_norm"].partition_broadcast(128), [], [gmix.d()])
            for s in range(nseq if "mixer" not in SKIP else 0):
                x1r = X1[s * SEQ:(s + 1) * SEQ, :]
                with contextlib.ExitStack() as e3:
                    tmp = {"xt": [sb(e3, "u%d_xt%d" % (s, i), [128, D], F32) for i in range(4)], "xs": [sb(e3, "u%d_xs%d" % (s, i), [128, D], BF16) for i in range(4)],
                           "st": [sb(e3, "u%d_st%d" % (s, i), [128, 8], F32) for i in range(4)], "junk": sb(e3, "u%d_junk" % s, [128, D], BF16)}
                    PA.mode = "dn"
                    for t in range(16):
                        norm_rows_T(P, O, nc, PA, C, tmp, x1r[t * 128:(t + 1) * 128, :], gmix, uT, t * 128, t)
                    PA.mode = "att"
                    P.barrier()
                if "nsa" not in SKIP:
                    nsa_phase(P, O, nc, PA, C, W, uT, obr[0], s)
                with contextlib.ExitStack() as e_pf:
                    winD = sb(e_pf, "winD_%d" % s, [128, 8, 2056], BF16)

                    def pf(winD=winD):
                        w_inv = W["w_in"].rearrange("(c p) f -> p c f", p=128)
                        for c in range(8):
                            O.DMA("pool", winD[:, c, :], w_inv[:, c, 1304:3360], [], [winD.d(c)])
                    mem_phase(P, O, nc, PA, C, W, uT, obr[2], mem[s * 256:(s + 1) * 256, :], after_weights=pf)
                    PA.mode = "dn"
                    dn_phase(P, O, nc, PA, C, W, uT, obr[1], winD_pre=winD)
                    PA.mode = "att"
                    P.barrier()
                if dbg and s == 0:
                    for k in range(3):
                        O.DMA("sp", dbo[k], obr[k][:, :, :], [obr[k].d(i) for i in range(16)], [], is_output=True)
                merge_phase(P, O, nc, PA, C, W, uT, obr, x1r, X2[s * SEQ:(s + 1) * SEQ, :], ps)
            P.barrier()
        with contextlib.ExitStack() as e4:
            f = F["ffn2"]
            ffn_phase(P, nc, e4, X2, out, f["wg"], f["wu"], f["wd"], f["pre"], f["post"], ident, ps, ntok, "f2_")
        P.barrier()
        P.finish()
        P.emit()
    return nc


def make_in_map(inputs, b0, nseq, consts):
    m = dict(consts)
    m["x"] = np.ascontiguousarray(inputs["x"][b0:b0 + nseq]).reshape(nseq * SEQ, D)
    m["mem"] = np.ascontiguousarray(inputs["mem"][b0:b0 + nseq]).reshape(nseq * 256, D)
    for p in ("ffn1", "ffn2"):
        m[p + "_pre_norm"] = np.ascontiguousarray(inputs[p + "_pre_norm"]).reshape(1, D)
        m[p + "_post_norm"] = np.ascontiguousarray(inputs[p + "_post_norm"]).reshape(1, D)
        for k in ("_w_gate", "_w_up", "_w_down"):
            m[p + k] = np.ascontiguousarray(inputs[p + k][0])
    for k in ("w_in", "cmp_k_w1", "cmp_v_w1", "cmp_k_w2", "cmp_v_w2", "w_mem_kv", "w_branch_nsa", "w_branch_dn", "w_branch_mem",
              "w_branch_gate", "w_out"):
        m[k] = np.ascontiguousarray(inputs[k][0])
    for k in ("dn_a_log", "dn_dt_bias", "dn_out_norm", "mem_norm", "mix_post_norm", "mix_pre_norm"):
        m[k] = np.ascontiguousarray(inputs[k]).reshape(1, -1)
    m["cmp_pos_kT"] = np.ascontiguousarray(inputs["cmp_pos_k"][0].T)
    m["cmp_pos_vT"] = np.ascontiguousarray(inputs["cmp_pos_v"][0].T)
    m["dn_conv_wT"] = np.ascontiguousarray(np.asarray(inputs["dn_conv_w"][0]).reshape(4, 12, 128).transpose(2, 1, 0))
    m["rel_bias"] = np.ascontiguousarray(inputs["rel_bias"]).reshape(1, 256)
    return m


def kernel(**inputs):
    inputs = {k: np.asarray(v, dtype=np.float32) for k, v in inputs.items()}
    n = 8
    nseq = 32 // n
    nc = build_nc(nseq)
    consts = host_consts()
    in_maps = [make_in_map(inputs, c * nseq, nseq, consts) for c in range(n)]
    res = run_bass_kernel_spmd(nc, in_maps, core_ids=list(range(n)))
    outs = [np.asarray(r["out"]).reshape(nseq, SEQ, D) for r in res.results]
    return np.concatenate(outs, axis=0).astype(np.float32)
```
